# Optimizing a Trainium2 kernel written in Bass

```python
import math
import jax, jax.numpy as jnp
from jax import lax
import numpy as np

D_MODEL = 1024
BATCH = 4
SEQ = 8192
DEPTH = 1

CHUNK = 64
Q_BLOCK = 128
EPS = 1e-6
D_FF = 2816
N_MOD = 9

GDN_HEADS = 4
GDN_DK = 128
GDN_DV = 128
CONV_K = 4

MLA_HEADS = 4
MLA_NOPE = 128
MLA_ROPE = 64
MLA_V = 128
MLA_Q_LORA = 384
MLA_KV_LORA = 256
ROPE_BASE = 10000.0

GDN_WIDTH = GDN_HEADS * GDN_DV
MLA_WIDTH = MLA_HEADS * MLA_V
MIX_WIDTH = GDN_WIDTH + MLA_WIDTH

IN_SPLITS = (GDN_HEADS * GDN_DK,
             GDN_HEADS * GDN_DK,
             GDN_WIDTH,
             GDN_WIDTH,
             GDN_HEADS,
             GDN_HEADS,
             MLA_Q_LORA,
             MLA_KV_LORA,
             MLA_ROPE)
N_IN = sum(IN_SPLITS)
IN_OFFSETS = tuple(int(o) for o in np.cumsum(IN_SPLITS)[:-1])

kernel_name = "hybrid_gdn_mla_macaron_adaln_block"


def _rms(x, w=None):
    xf = x.astype(jnp.float32)
    y = xf * lax.rsqrt(jnp.mean(xf * xf, axis=-1, keepdims=True) + EPS)
    if w is not None:
        y = y * w.astype(jnp.float32)
    return y.astype(x.dtype)


def _l2n(x):
    return x * lax.rsqrt(jnp.sum(x * x, axis=-1, keepdims=True) + EPS)


def _modulate(x, shift, scale):
    return _rms(x) * (1.0 + scale[:, None, :]) + shift[:, None, :]


def _swiglu(h, w_in, w_out):
    gate, up = jnp.split(h @ w_in, 2, axis=-1)
    return (jax.nn.silu(gate) * up) @ w_out


def _rope(x, cos, sin):
    x1, x2 = jnp.split(x, 2, axis=-1)
    return jnp.concatenate([x1 * cos - x2 * sin, x2 * cos + x1 * sin], axis=-1)


def _causal_conv(x, w):
    return lax.conv_general_dilated(
        x, w[:, None, :].astype(x.dtype), window_strides=(1,),
        padding=[(CONV_K - 1, 0)], dimension_numbers=("NWC", "WIO", "NWC"),
        feature_group_count=x.shape[-1])


def _gated_delta_rule(q, k, v, g, beta):
    B, S, H, _ = q.shape
    nc = S // CHUNK

    def to_chunks(t):
        return t.reshape(B, nc, CHUNK, H, t.shape[-1]).transpose(0, 3, 1, 2, 4)

    q, k, v = to_chunks(q), to_chunks(k), to_chunks(v)
    g = g.reshape(B, nc, CHUNK, H).transpose(0, 3, 1, 2)
    beta = beta.reshape(B, nc, CHUNK, H).transpose(0, 3, 1, 2)

    G = jnp.cumsum(g, axis=-1)
    idx = jnp.arange(CHUNK)
    incl = idx[:, None] >= idx[None, :]
    strict = idx[:, None] > idx[None, :]
    decay = jnp.exp(jnp.where(incl, G[..., :, None] - G[..., None, :], -jnp.inf))

    kk = jnp.einsum('bhncd,bhnsd->bhncs', k, k)
    A = jnp.where(strict, beta[..., :, None] * kk * decay, 0.0)
    M = A + jnp.eye(CHUNK, dtype=A.dtype)
    rhs = jnp.concatenate([v * beta[..., None], k * (beta * jnp.exp(G))[..., None]], axis=-1)
    W = lax.linalg.triangular_solve(M, rhs, left_side=True, lower=True, unit_diagonal=True)
    u, wk = W[..., :GDN_DV], W[..., GDN_DV:]

    qk = jnp.einsum('bhncd,bhnsd->bhncs', q, k) * decay
    q_dec = q * jnp.exp(G)[..., None]
    k_dec = k * jnp.exp(G[..., -1:] - G)[..., None]
    g_last = jnp.exp(G[..., -1])

    xs = tuple(jnp.moveaxis(t, 2, 0) for t in (u, wk, q_dec, k_dec, qk, g_last))

    def step(state, inp):
        u_c, wk_c, qd_c, kd_c, qk_c, gl_c = inp
        v_new = u_c - jnp.einsum('bhck,bhkv->bhcv', wk_c, state)
        o_c = jnp.einsum('bhck,bhkv->bhcv', qd_c, state) + jnp.einsum('bhcs,bhsv->bhcv', qk_c, v_new)
        state = state * gl_c[..., None, None] + jnp.einsum('bhck,bhcv->bhkv', kd_c, v_new)
        return state, o_c

    s0 = jnp.zeros((B, H, GDN_DK, GDN_DV), jnp.float32)
    _, o = lax.scan(step, s0, xs)
    return o.transpose(1, 0, 3, 2, 4).reshape(B, S, H, GDN_DV)


def _hybrid_mixer(h, cos, sin, w_in, gdn_conv_w, gdn_a_log, gdn_dt_bias, gdn_norm_w,
                  mla_q_norm_w, mla_w_uq, mla_kv_norm_w, mla_w_ukv,
                  qkn_q_nope, qkn_q_rope, qkn_k_nope, qkn_k_rope, mla_out_norm_w, w_out):
    B, S, _ = h.shape
    nb = S // Q_BLOCK
    proj = h @ w_in
    gq, gk, gv, gz, ga, gb, cq, ckv, kr = jnp.split(proj, IN_OFFSETS, axis=-1)

    qkv = jax.nn.silu(_causal_conv(jnp.concatenate([gq, gk, gv], axis=-1), gdn_conv_w))
    q_a = qkv[..., :GDN_HEADS * GDN_DK].reshape(B, S, GDN_HEADS, GDN_DK).astype(jnp.float32)
    k_a = qkv[..., GDN_HEADS * GDN_DK:2 * GDN_HEADS * GDN_DK].reshape(B, S, GDN_HEADS, GDN_DK).astype(jnp.float32)
    v_a = qkv[..., 2 * GDN_HEADS * GDN_DK:].reshape(B, S, GDN_HEADS, GDN_DV).astype(jnp.float32)
    q_a = _l2n(q_a) * (GDN_DK ** -0.5)
    k_a = _l2n(k_a)
    beta = jax.nn.sigmoid(gb.astype(jnp.float32))
    g = -jnp.exp(gdn_a_log.astype(jnp.float32)) * jax.nn.softplus(
        ga.astype(jnp.float32) + gdn_dt_bias.astype(jnp.float32))
    o_a = _gated_delta_rule(q_a, k_a, v_a, g, beta).astype(h.dtype)
    o_a = _rms(o_a, gdn_norm_w) * jax.nn.silu(gz.reshape(B, S, GDN_HEADS, GDN_DV))

    qf = (_rms(cq, mla_q_norm_w) @ mla_w_uq).reshape(B, S, MLA_HEADS, MLA_NOPE + MLA_ROPE)
    kvf = (_rms(ckv, mla_kv_norm_w) @ mla_w_ukv).reshape(B, S, MLA_HEADS, MLA_NOPE + MLA_V)
    q_nope, q_rope = qf[..., :MLA_NOPE], qf[..., MLA_NOPE:]
    k_nope, v_b = kvf[..., :MLA_NOPE], kvf[..., MLA_NOPE:]
    scale = (MLA_NOPE + MLA_ROPE) ** -0.5
    q_nope = _rms(q_nope, qkn_q_nope) * scale
    q_rope = _rope(_rms(q_rope, qkn_q_rope), cos[:, :, None], sin[:, :, None]) * scale
    k_nope = _rms(k_nope, qkn_k_nope)
    k_rope = _rope(_rms(kr, qkn_k_rope), cos, sin)

    k_chunk = jnp.arange(S) // CHUNK
    q_chunk_b = k_chunk.reshape(nb, Q_BLOCK)
    qn_b = q_nope.reshape(B, nb, Q_BLOCK, MLA_HEADS, MLA_NOPE).transpose(1, 0, 2, 3, 4)
    qr_b = q_rope.reshape(B, nb, Q_BLOCK, MLA_HEADS, MLA_ROPE).transpose(1, 0, 2, 3, 4)

    def attend(blk):
        qn, qr, qc = blk
        s = (jnp.einsum('bqhd,bkhd->bhqk', qn, k_nope)
             + jnp.einsum('bqhd,bkd->bhqk', qr, k_rope)).astype(jnp.float32)
        s = jnp.where(qc[:, None] >= k_chunk[None, :], s, -jnp.inf)
        p = jax.nn.softmax(s, axis=-1).astype(v_b.dtype)
        return jnp.einsum('bhqk,bkhd->bqhd', p, v_b)

    o_b = lax.map(attend, (qn_b, qr_b, q_chunk_b))
    o_b = o_b.transpose(1, 0, 2, 3, 4).reshape(B, S, MLA_HEADS, MLA_V)
    o_b = _rms(o_b, mla_out_norm_w)

    mixed = jnp.concatenate([o_a.reshape(B, S, GDN_WIDTH), o_b.reshape(B, S, MLA_WIDTH)], axis=-1)
    return mixed @ w_out


def setup_inputs(seed: int = 0) -> dict:
    key = jax.random.key(seed)
    ks = jax.random.split(key, 32)
    f32 = jnp.float32
    L = DEPTH

    def nrm(k, shape, fan_in, mult=1.0):
        return jax.random.normal(k, shape, f32) * (mult * fan_in ** -0.5)

    def gain(k, shape):
        return 1.0 + 0.02 * jax.random.normal(k, shape, f32)

    x = jax.random.normal(ks[0], (BATCH, SEQ, D_MODEL), f32)
    c = jax.random.normal(ks[1], (BATCH, D_MODEL), f32)
    offset = jax.random.randint(ks[2], (BATCH, 1), 0, 4096, dtype=jnp.int32)
    positions = (offset + jnp.arange(SEQ, dtype=jnp.int32)[None, :]).astype(jnp.int32)

    dt = jnp.exp(jax.random.uniform(ks[10], (L, GDN_HEADS), f32, math.log(1e-3), math.log(1e-1)))
    return {
        "x": x,
        "c": c,
        "positions": positions,
        "w_ada": nrm(ks[3], (L, D_MODEL, N_MOD * D_MODEL), D_MODEL, 0.5),
        "b_ada": 0.02 * jax.random.normal(ks[4], (L, N_MOD * D_MODEL), f32),
        "ffn1_w_in": nrm(ks[5], (L, D_MODEL, 2 * D_FF), D_MODEL),
        "ffn1_w_out": nrm(ks[6], (L, D_FF, D_MODEL), D_FF),
        "w_in": nrm(ks[7], (L, D_MODEL, N_IN), D_MODEL),
        "gdn_conv_w": nrm(ks[8], (L, CONV_K, 3 * GDN_WIDTH), CONV_K),
        "gdn_a_log": jnp.log(jax.random.uniform(ks[9], (L, GDN_HEADS), f32, 1.0, 16.0)),
        "gdn_dt_bias": dt + jnp.log(-jnp.expm1(-dt)),
        "gdn_norm_w": gain(ks[11], (L, GDN_DV)),
        "mla_q_norm_w": gain(ks[12], (L, MLA_Q_LORA)),
        "mla_w_uq": nrm(ks[13], (L, MLA_Q_LORA, MLA_HEADS * (MLA_NOPE + MLA_ROPE)), MLA_Q_LORA),
        "mla_kv_norm_w": gain(ks[14], (L, MLA_KV_LORA)),
        "mla_w_ukv": nrm(ks[15], (L, MLA_KV_LORA, MLA_HEADS * (MLA_NOPE + MLA_V)), MLA_KV_LORA),
        "qkn_q_nope": gain(ks[16], (L, MLA_NOPE)),
        "qkn_q_rope": gain(ks[17], (L, MLA_ROPE)),
        "qkn_k_nope": gain(ks[18], (L, MLA_NOPE)),
        "qkn_k_rope": gain(ks[19], (L, MLA_ROPE)),
        "mla_out_norm_w": gain(ks[20], (L, MLA_V)),
        "w_out": nrm(ks[21], (L, MIX_WIDTH, D_MODEL), MIX_WIDTH),
        "ffn2_w_in": nrm(ks[22], (L, D_MODEL, 2 * D_FF), D_MODEL),
        "ffn2_w_out": nrm(ks[23], (L, D_FF, D_MODEL), D_FF),
    }


def reference(x, c, positions, w_ada, b_ada, ffn1_w_in, ffn1_w_out, w_in, gdn_conv_w,
              gdn_a_log, gdn_dt_bias, gdn_norm_w, mla_q_norm_w, mla_w_uq, mla_kv_norm_w,
              mla_w_ukv, qkn_q_nope, qkn_q_rope, qkn_k_nope, qkn_k_rope, mla_out_norm_w,
              w_out, ffn2_w_in, ffn2_w_out):
    half = MLA_ROPE // 2
    inv_freq = ROPE_BASE ** (-jnp.arange(half, dtype=jnp.float32) / half)
    ang = positions.astype(jnp.float32)[..., None] * inv_freq
    cos = jnp.cos(ang).astype(x.dtype)
    sin = jnp.sin(ang).astype(x.dtype)
    sc = jax.nn.silu(c)

    for l in range(DEPTH):
        mod = sc @ w_ada[l] + b_ada[l]
        sh1, s1, g1, sh2, s2, g2, sh3, s3, g3 = jnp.split(mod, N_MOD, axis=-1)
        h = _modulate(x, sh1, s1)
        x = x + 0.5 * g1[:, None, :] * _swiglu(h, ffn1_w_in[l], ffn1_w_out[l])
        h = _modulate(x, sh2, s2)
        y = _hybrid_mixer(h, cos, sin, w_in[l], gdn_conv_w[l], gdn_a_log[l], gdn_dt_bias[l],
                          gdn_norm_w[l], mla_q_norm_w[l], mla_w_uq[l], mla_kv_norm_w[l],
                          mla_w_ukv[l], qkn_q_nope[l], qkn_q_rope[l], qkn_k_nope[l],
                          qkn_k_rope[l], mla_out_norm_w[l], w_out[l])
        x = x + g2[:, None, :] * y
        h = _modulate(x, sh3, s3)
        x = x + 0.5 * g3[:, None, :] * _swiglu(h, ffn2_w_in[l], ffn2_w_out[l])
    return x
```

```python
import numpy as np
from contextlib import ExitStack
import concourse.bass as bass
import concourse.mybir as mybir
from concourse.bass_utils import run_bass_kernel_spmd

F32 = mybir.dt.float32
BF16 = mybir.dt.bfloat16
I32 = mybir.dt.int32
AF = mybir.ActivationFunctionType
ALU = mybir.AluOpType

SAME_ENGINE_SYNC = True


class KB:
    NDS = 12

    def __init__(self, nc, es):
        self.nc, self.es = nc, es
        self.engobj = {'pe': nc.tensor, 'act': nc.scalar, 'dve': nc.vector,
                       'pool': nc.gpsimd, 'sp': nc.sync}
        self.prog = {e: [] for e in self.engobj}
        self.sems, self.cnt = {}, {}
        for e in self.engobj:
            self.sems[e] = es.enter_context(nc.semaphore("s_" + e))
            self.cnt[e] = 0
        for q in ('sp', 'pool'):
            for i in range(self.NDS):
                key = ('d', q, i)
                self.sems[key] = es.enter_context(nc.semaphore("d_%s_%d" % (q, i)))
                self.cnt[key] = 0
        self.dnext = {'sp': 0, 'pool': 0}
        self.waited = {e: {} for e in self.engobj}
        self.last_w, self.readers = {}, {}
        self.ninst = 0
        self.flag = {e: set() for e in self.engobj}
        self.rank = None

    def sb(self, name, shape, dt):
        return self.es.enter_context(self.nc.sbuf_tensor(name, list(shape), dt))

    def ps(self, name, shape, dt):
        return self.es.enter_context(self.nc.psum_tensor(name, list(shape), dt))

    def _deps(self, reads, writes):
        deps = []
        for b in reads:
            if b in self.last_w:
                deps.append(self.last_w[b])
        for b in writes:
            if b in self.last_w:
                deps.append(self.last_w[b])
            deps.extend(self.readers.get(b, ()))
        return deps

    def _waits(self, eng, deps):
        need = {}
        for key, val in deps:
            if key == eng and (eng == 'pe' or not SAME_ENGINE_SYNC):
                continue
            if self.waited[eng].get(key, 0) >= val:
                continue
            if need.get(key, 0) < val:
                need[key] = val
        for key, val in need.items():
            self.waited[eng][key] = val
            sem, eo = self.sems[key], self.engobj[eng]
            if key in self.engobj:
                self.flag[key].add(val)
                self.prog[eng].append(lambda eo=eo, sem=sem, key=key, val=val: eo.wait_ge(sem, self.rank[key][val]))
            else:
                self.prog[eng].append(lambda eo=eo, sem=sem, val=val: eo.wait_ge(sem, val))

    def _record(self, stamp, reads, writes):
        for b in writes:
            self.last_w[b] = stamp
            self.readers[b] = []
        for b in reads:
            if b in writes:
                continue
            self.readers.setdefault(b, []).append(stamp)

    muted = False
    capture = None
    cur = None

    def capture_start(self):
        self.capture = []
        self.cur = None

    def task(self, tag):
        if self.capture is not None:
            self.cur = []
            self.capture.append((tag, self.cur))

    def flush(self):
        tasks, self.capture, self.cur = self.capture, None, None
        if not tasks:
            return
        g = [t for tag, t in tasks if tag == 'g']
        m = [t for tag, t in tasks if tag == 'm']
        G = [c for t in g for c in t]
        M = [c for t in m for c in t]
        merged = []
        mi = 0
        for gi, c in enumerate(G):
            merged.append(c)
            while mi < len(M) and (mi + 1) * len(G) <= (gi + 1) * len(M):
                merged.append(M[mi]); mi += 1
        merged.extend(M[mi:])
        self.replay(merged)

    def replay(self, t):
        for c in t:
            if c[0] == 'I':
                self.I(c[1], c[2], *c[3], reads=c[4], writes=c[5], **c[6])
            else:
                self.dma(c[1], c[2], c[3], reads=c[4], writes=c[5])

    keymap = {}

    @staticmethod
    def merge(streams):
        items = []
        for si, st in enumerate(streams):
            n = len(st)
            for i, c in enumerate(st):
                items.append(((i + 0.5) / n, si, i, c))
        items.sort(key=lambda t: (t[0], t[1], t[2]))
        return [t[3] for t in items]

    def I(self, eng, op, *args, reads=(), writes=(), **kw):
        if self.muted:
            return
        if self.keymap:
            reads = tuple(self.keymap.get(x, x) for x in reads); writes = tuple(self.keymap.get(x, x) for x in writes)
        if self.capture is not None:
            self.cur.append(('I', eng, op, args, tuple(reads), tuple(writes), kw))
            return
        deps = self._deps(reads, writes)
        banks = self._psum_banks(list(args) + list(kw.values()))
        for bk in banks:
            for e2, idx2 in self.bank_last.get(bk, {}).items():
                if e2 != eng:
                    deps.append((e2, idx2))
        self._waits(eng, deps)
        self.cnt[eng] += 1
        eo, sem = self.engobj[eng], self.sems[eng]

        def emit(eo=eo, sem=sem, op=op, args=args, kw=kw, eng=eng, idx=self.cnt[eng]):
            inst = getattr(eo, op)(*args, **kw)
            if idx in self.flag[eng]:
                inst.then_inc(sem, 1)
        self.prog[eng].append(emit)
        self._record((eng, self.cnt[eng]), reads, writes)
        for bk in banks:
            self.bank_last.setdefault(bk, {})[eng] = self.cnt[eng]
        self.ninst += 1

    bank_last = None

    def _psum_banks(self, objs):
        if self.bank_last is None:
            self.bank_last = {}
        out = set()
        for a in objs:
            if getattr(a, 'space', None) is None or str(a.space) != 'PSUM':
                continue
            sz = 2 if a.dtype == BF16 else 4
            lo = a.offset
            hi = lo + sum((n - 1) * abs(st) for st, n in list(a.ap)[1:])
            for b in range(lo * sz // 2048, hi * sz // 2048 + 1):
                out.add((a.name, b))
        return out

    def dma(self, q, out, in_, reads=(), writes=()):
        if self.muted:
            return
        if self.keymap:
            reads = tuple(self.keymap.get(x, x) for x in reads); writes = tuple(self.keymap.get(x, x) for x in writes)
        if self.capture is not None:
            self.cur.append(('D', q, out, in_, tuple(reads), tuple(writes)))
            return
        slot = self.dnext[q]
        self.dnext[q] = (slot + 1) % self.NDS
        key = ('d', q, slot)
        deps = self._deps(reads, writes)
        if self.cnt[key]:
            deps.append((key, self.cnt[key]))
        self._waits(q, deps)
        self.cnt[key] += 16
        eo, sem = self.engobj[q], self.sems[key]

        def emit(eo=eo, sem=sem, out=out, in_=in_):
            eo.dma_start(out=out, in_=in_).then_inc(sem, 16)
        self.prog[q].append(emit)
        self._record((key, self.cnt[key]), reads, writes)
        self.ninst += 1

    def barrier(self):
        deps = [(key, val) for key, val in self.cnt.items() if val]
        for e in self.engobj:
            self._waits(e, [(k_, v) for (k_, v) in deps if k_ != e])

    def finish(self, out_keys):
        deps = [self.last_w[b] for b in out_keys if b in self.last_w]
        self._waits('sp', deps)
        self.rank = {e: {idx: i + 1 for i, idx in enumerate(sorted(self.flag[e]))} for e in self.engobj}
        nc = self.nc
        with nc.allow_non_contiguous_dma(reason="small strided layout loads"), \
                nc.allow_low_precision("bf16 matmul operands, fp32 accumulation"):
            with nc.Block() as block:
                @block.tensor
                def _(e):
                    for f in self.prog['pe']:
                        f()

                @block.scalar
                def _(e):
                    for f in self.prog['act']:
                        f()

                @block.vector
                def _(e):
                    for f in self.prog['dve']:
                        f()

                @block.gpsimd
                def _(e):
                    for f in self.prog['pool']:
                        f()

                @block.sync
                def _(e):
                    for f in self.prog['sp']:
                        f()


D = 1024
DFF = 2816
NFC = DFF // 128
EPS = 1e-6
NIN = 2760
OFF_Q, OFF_K, OFF_V, OFF_Z, OFF_A, OFF_B, OFF_CQ, OFF_CKV, OFF_KR = 0, 512, 1024, 1536, 2048, 2052, 2056, 2440, 2696
NEG = -30000.0
TWO_PI = 6.283185307179586


import os
P2CUT = int(os.environ.get('P2CUT', '99'))


def build_program(NB, stop_after=99, G=2, GDT=F32):
    nc = bass.Bass("TRN2", target_bir_lowering=False)
    NT = NB * 128
    OWN = [j for j in range(NB) if j % 2 == 0]
    NOWN = len(OWN)

    def din(name, shape, dt=F32):
        return nc.dram_tensor(name, list(shape), dt, kind="ExternalInput").ap()

    def dscr(name, shape, dt):
        return nc.dram_tensor(name, list(shape), dt).ap()

    xs = din("xs", [NT, D])
    c_col = din("c_col", [128, 8])
    pos_col = din("pos_col", [128, NB], I32)
    flags = din("flags", [128, 2])
    w_ada = din("w_ada", [D, 9 * D])
    b_adaT = din("b_adaT", [128, 72])
    b_gate = din("b_gate", [1, 3 * D])
    w1i = din("ffn1_w_in", [D, 2 * DFF]); w1o = din("ffn1_w_out", [DFF, D])
    w2i = din("ffn2_w_in", [D, 2 * DFF]); w2o = din("ffn2_w_out", [DFF, D])
    w_in = din("w_in", [D, NIN])
    conv_c = din("conv_c", [128, 12, 4])
    alog_bc = din("alog_bc", [128, 4]); dtb_bc = din("dtb_bc", [128, 4])
    gnw_bc = din("gnw_bc", [128, 512])
    qnw_col = din("qnw_col", [128, 3]); kvnw_col = din("kvnw_col", [128, 2])
    w_uq = din("w_uq", [384, 768]); w_ukv = din("w_ukv", [256, 1024])
    gqn_col = din("gqn_col", [128, 1]); gkn_col = din("gkn_col", [128, 1])
    gqr_bc = din("gqr_bc", [128, 64]); gkr_bc = din("gkr_bc", [128, 64])
    onw_bc = din("onw_bc", [128, 512])
    w_o = din("w_out", [D, D])
    c_ident = din("c_ident", [128, 128]); c_utri = din("c_utri", [128, 128])
    c_maskU = din("c_maskU", [128, 128]); c_maskL = din("c_maskL", [128, 128])
    c_bones = din("c_bones", [128, 128]); c_cind = din("c_cind", [128, 256])
    c_invf = din("c_invf", [128, 32]); c_dmask = din("c_dmask", [128, 512])
    out = nc.dram_tensor("out", [NOWN * 128, D], F32, kind="ExternalOutput").ap()

    gB_d = dscr("gB_d", [3, 128, D], F32)
    x1_d = dscr("x1_d", [NOWN, 128, D], F32)
    h2T_d = dscr("h2T_d", [NB, 128, 8 * 128], BF16)
    KT_d = dscr("KT_d", [NB, 128, 512], BF16)
    V_d = dscr("V_d", [NB, 128, 4 * 130], BF16)
    krT_d = dscr("krT_d", [NB, 64, 128], BF16)
    QTn_d = dscr("QTn_d", [NOWN, 128, 512], BF16)
    QTr_d = dscr("QTr_d", [NOWN, 64, 512], BF16)
    oa_d = dscr("oa_d", [NOWN, 128, 512], BF16)
    ob_d = dscr("ob_d", [NOWN, 128, 512], BF16)
    dbg = {}

    with ExitStack() as es:
        k = KB(nc, es)
        I, dma, sb, ps = k.I, k.dma, k.sb, k.ps

        ident_f = sb("ident_f", [128, 128], F32); ident_b = sb("ident_b", [128, 128], BF16)
        ones_f = sb("ones_f", [128, 128], F32)
        modT = sb("modT", [128, 72], F32); s1p = sb("s1p", [128, 72], F32)
        flg = sb("flg", [128, 2], F32)
        dma('sp', ident_f[:], c_ident, writes=['ident_f'])
        dma('pool', ident_b[:], c_ident, writes=['ident_b'])
        dma('sp', flg[:], flags, writes=['flg'])
        I('dve', 'memset', ones_f[:], 1.0, writes=['ones_f'])

        def rstd_from_ss(ss_ap, out_ap, n, keys_r, keys_w, tmp_ap):
            w_ = ss_ap.shape[1]
            I('dve', 'tensor_scalar', tmp_ap, ss_ap, 1.0 / n, EPS, ALU.mult, ALU.add, reads=keys_r, writes=keys_w)
            I('pool', 'tensor_tensor', out_ap, tmp_ap, mhalf[:, 0:w_], ALU.pow, reads=keys_w + ['mhalf'], writes=keys_w)

        eps_col = sb("eps_col", [128, 1], F32)
        mhalf = sb("mhalf", [128, 8], F32)
        I('dve', 'memset', mhalf[:], -0.5, writes=['mhalf'])
        I('dve', 'memset', eps_col[:], EPS, writes=['eps_col'])

        with ExitStack() as p0:
            es0 = k.es; k.es = p0
            ccol = sb("ccol", [128, 8], F32); scb = sb("scb", [128, 8], BF16)
            wA = [sb("wA%d" % i, [128, 8, 1024], BF16) for i in range(2)]
            badT = sb("badT", [128, 72], F32)
            bg = sb("bg", [1, 3 * D], F32); grow = sb("grow", [1, D], F32); gBt = sb("gBt", [128, D], F32)
            pmod = ps("pmod", [128, 512], F32); prow = ps("prow", [1, 512], F32); pB = ps("pB", [128, 512], F32)
            dma('sp', ccol[:], c_col, writes=['ccol'])
            dma('sp', badT[:], b_adaT, writes=['badT'])
            dma('sp', bg[:], b_gate, writes=['bg'])
            I('act', 'activation', scb[:], ccol[:], AF.Silu, reads=['ccol'], writes=['scb'])
            wsrc = w_ada.rearrange("(kc p) n -> p kc n", p=128)
            for cg in range(9):
                wt, wk_ = wA[cg % 2], 'wA%d' % (cg % 2)
                dma('pool', wt[:], wsrc[:, :, cg * 1024:(cg + 1) * 1024], writes=[wk_])
                for fc in range(8):
                    for kc in range(8):
                        I('pe', 'matmul', pmod[:, cg * 8 + fc: cg * 8 + fc + 1], wt[:, kc, fc * 128:(fc + 1) * 128],
                          scb[:, kc:kc + 1], start=(kc == 0), stop=(kc == 7), reads=[wk_, 'scb'], writes=['pmod'])
                if cg % 3 == 2:
                    gi = cg // 3
                    for half in range(2):
                        for kc in range(8):
                            I('pe', 'matmul', prow[:, :], scb[:, kc:kc + 1], wt[:, kc, half * 512:(half + 1) * 512],
                              start=(kc == 0), stop=(kc == 7), reads=[wk_, 'scb'], writes=['prow'])
                        I('dve', 'tensor_tensor', grow[:, half * 512:(half + 1) * 512], prow[:, :],
                          bg[:, gi * D + half * 512: gi * D + (half + 1) * 512], ALU.add, reads=['prow', 'bg'], writes=['grow'])
                    for half in range(2):
                        I('pe', 'matmul', pB[:, :], ones_f[0:1, :], grow[0:1, half * 512:(half + 1) * 512], start=True, stop=True,
                          reads=['ones_f', 'grow'], writes=['pB'])
                        I('act', 'activation', gBt[:, half * 512:(half + 1) * 512], pB[:, :], AF.Identity,
                          scale=(1.0 if gi == 1 else 0.5), reads=['pB'], writes=['gBt'])
                    dma('sp', gB_d[gi], gBt[:], reads=['gBt'], writes=['gB_d%d' % gi])
            I('dve', 'tensor_tensor', modT[:], pmod[:, 0:72], badT[:], ALU.add, reads=['pmod', 'badT'], writes=['modT'])
            I('dve', 'tensor_scalar', s1p[:], modT[:], 1.0, None, ALU.add, reads=['modT'], writes=['s1p'])
            k.es = es0
        k.barrier()
        if stop_after == 0:
            dbg['modT'] = (modT, [128, 72], F32)

        def SH(n):
            return modT[:, (3 * n) * 8:(3 * n) * 8 + 8]

        def SC(n):
            return s1p[:, (3 * n + 1) * 8:(3 * n + 1) * 8 + 8]

        def load_w_bf16(tile, key, src, rows_chunks, cols, c0=0, piece=1024):
            s = src.rearrange("(kc p) n -> p kc n", p=128)
            for a in range(0, cols, piece):
                b = min(cols, a + piece)
                dma('pool', tile[:, :, a:b], s[:, :, c0 + a:c0 + b], writes=[key])

        def norm_T(xap, nrm, hT_ap_fn, ss_ap, tmp_ap, rstd_ap, xn, xnk, junk, pT, pTk, hTk, statk):
            I('act', 'activation', junk[:], xap, AF.Square, accum_out=ss_ap, reads=[statk + '_x'], writes=['junk', statk])
            rstd_from_ss(ss_ap, rstd_ap, D, [statk], [statk], tmp_ap)
            I('dve', 'tensor_scalar', xn[:], xap, rstd_ap, None, ALU.mult, reads=[statk, statk + '_x'], writes=[xnk])
            for kc in range(8):
                I('pe', 'transpose', pT[:, kc, :], xn[:, kc * 128:(kc + 1) * 128], ident_b[:], reads=[xnk, 'ident_b'], writes=[pTk])
            for kc in range(8):
                I('act', 'activation', hT_ap_fn(kc), pT[:, kc, :], AF.Identity, scale=SC(nrm)[:, kc:kc + 1],
                  bias=SH(nrm)[:, kc:kc + 1], reads=[pTk, 's1p', 'modT'], writes=[hTk])

        def ffn_group(hT, hTk, GT, Wi, Wik, aT, pgs, pus, sgs, it, fcs=None):
            for fc in (range(NFC) if fcs is None else fcs):
                b = (it * NFC + fc) % 2
                pg, pu, sg = pgs[b], pus[b], sgs[b]
                for kc in range(8):
                    I('pe', 'matmul', pg[:, 0:GT], Wi[:, kc, fc * 128:(fc + 1) * 128], hT[:, kc, 0:GT], start=(kc == 0), stop=(kc == 7),
                      reads=[Wik, hTk], writes=['pg%d' % b])
                for kc in range(8):
                    I('pe', 'matmul', pu[:, 0:GT], Wi[:, kc, DFF + fc * 128:DFF + (fc + 1) * 128], hT[:, kc, 0:GT], start=(kc == 0), stop=(kc == 7),
                      reads=[Wik, hTk], writes=['pu%d' % b])
                I('act', 'activation', sg[:, 0:GT], pg[:, 0:GT], AF.Silu, reads=['pg%d' % b], writes=['sg%d' % b])
                I('dve', 'tensor_tensor', aT[:, fc, 0:GT], sg[:, 0:GT], pu[:, 0:GT], ALU.mult, reads=['sg%d' % b, 'pu%d' % b], writes=['aT'])

        def ffn_down(aT, s, Wo, Wok, py):
            for half in range(2):
                for fc in range(NFC):
                    I('pe', 'matmul', py[:, half * 512:(half + 1) * 512], aT[:, fc, s * 128:(s + 1) * 128], Wo[:, fc, half * 512:(half + 1) * 512],
                      start=(fc == 0), stop=(fc == NFC - 1), reads=['aT', Wok], writes=['py'])

        GT_MAX = G * 128
        with ExitStack() as p1:
            es0 = k.es; k.es = p1
            Wi = sb("Wi", [128, 8, 2 * DFF], BF16); Wo = sb("Wo", [128, NFC, D], BF16)
            gB = sb("gB", [128, D], F32)
            xg = [sb("xg%d" % i, [128, G, D], F32) for i in range(2)]
            hT = sb("hT", [128, 8, GT_MAX], BF16)
            aT = sb("aT", [128, NFC, GT_MAX], BF16)
            h2o = [sb("h2o%d" % i, [128, 8, 128], BF16) for i in range(2)]
            xn = sb("xn", [128, D], BF16); junk = sb("junk", [128, D], BF16)
            yg = sb("yg", [128, D], F32)
            sgs = [sb("sg%d" % i, [128, GT_MAX], F32) for i in range(2)]
            st = sb("st", [128, 8], F32)
            pT = ps("pT", [128, 8, 128], BF16)
            pgs = [ps("pg%d" % i, [128, 512], F32) for i in range(2)]
            pus = [ps("pu%d" % i, [128, 512], F32) for i in range(2)]
            py = ps("py", [128, D], F32)
            load_w_bf16(Wi, 'Wi', w1i, 8, 2 * DFF)
            load_w_bf16(Wo, 'Wo', w1o, NFC, D)
            dma('sp', gB[:], gB_d[0], reads=['gB_d0'], writes=['gB'])
            groups = [list(range(a, min(NB, a + G))) for a in range(0, NB, G)] if stop_after >= 1 else []
            NG = len(groups)
            hTs = [hT, sb("hT_b", [128, 8, GT_MAX], BF16)]
            xnf = [sb("xnf%d" % i, [128, D], BF16) for i in range(G)]
            xn2 = [sb("xn2_%d" % i, [128, D], BF16) for i in range(G)]
            pTs = [pT, ps("pT_b", [128, 8, 128], BF16)]
            st12 = sb("st12", [128, 6 * G + 6], F32)

            def stats(xap, xk, c0, xn_t, xnk):
                I('act', 'activation', junk[:], xap, AF.Square, accum_out=st12[:, c0:c0 + 1], reads=[xk], writes=['junk', 'st12'])
                rstd_from_ss(st12[:, c0:c0 + 1], st12[:, c0 + 2:c0 + 3], D, ['st12'], ['st12'], st12[:, c0 + 1:c0 + 2])
                I('dve', 'tensor_scalar', xn_t[:], xap, st12[:, c0 + 2:c0 + 3], None, ALU.mult, reads=['st12', xk], writes=[xnk])

            def tr_evac(xn_t, xnk, nrm, pT_t, pTk, dst_fn, dstk):
                for kc in range(8):
                    I('pe', 'transpose', pT_t[:, kc, :], xn_t[:, kc * 128:(kc + 1) * 128], ident_b[:], reads=[xnk, 'ident_b'], writes=[pTk])
                for kc in range(8):
                    I('dve', 'tensor_scalar', dst_fn(kc), pT_t[:, kc, :], SC(nrm)[:, kc:kc + 1], SH(nrm)[:, kc:kc + 1], ALU.mult, ALU.add,
                      reads=[pTk, 's1p', 'modT'], writes=[dstk])

            k.capture_start()
            for gi_, blks in enumerate(groups):
                xt, xk = xg[gi_ % 2], 'xg%d' % (gi_ % 2)
                hTg, hTgk = hTs[gi_ % 2], 'hT%d' % (gi_ % 2)
                ng = len(blks); GT = ng * 128
                k.task(('L', gi_))
                dma('sp', xt[:, 0:ng, :], xs[blks[0] * 128:(blks[0] + ng) * 128, :].rearrange("(s p) d -> p s d", p=128), writes=[xk])
                k.task(('Fa', gi_))
                for s in range(ng):
                    stats(xt[:, s, :], xk, 3 * s, xnf[s], 'xnf%d' % s)
                k.task(('Fb', gi_))
                for s in range(ng):
                    tr_evac(xnf[s], 'xnf%d' % s, 0, pTs[s], 'pTs%d' % s, lambda kc, s=s, hTg=hTg: hTg[:, kc, s * 128:(s + 1) * 128], hTgk)
                k.task(('GU1', gi_))
                ffn_group(hTg, hTgk, GT, Wi, 'Wi', aT, pgs, pus, sgs, gi_, fcs=range(0, NFC // 2))
                k.task(('GU2', gi_))
                ffn_group(hTg, hTgk, GT, Wi, 'Wi', aT, pgs, pus, sgs, gi_, fcs=range(NFC // 2, NFC))
                k.task(('D', gi_))
                for s in range(ng):
                    blk = blks[s]
                    ffn_down(aT, s, Wo, 'Wo', py)
                    I('dve', 'tensor_tensor', yg[:], py[:], gB[:], ALU.mult, reads=['py', 'gB'], writes=['yg'])
                    I('dve', 'tensor_tensor', xt[:, s, :], xt[:, s, :], yg[:], ALU.add, reads=['yg', xk], writes=[xk])
                    if blk % 2 == 0:
                        dma('sp', x1_d[blk // 2], xt[:, s, :], reads=[xk], writes=['x1_d%d' % (blk // 2)])
                k.task(('Na', gi_))
                for s in range(ng):
                    stats(xt[:, s, :], xk, 3 * G + 3 * s, xn2[s], 'xn2_%d' % s)
                k.task(('Nb', gi_))
                for s in range(ng):
                    blk = blks[s]
                    ho, hk = h2o[s % 2], 'h2o%d' % (s % 2)
                    tr_evac(xn2[s], 'xn2_%d' % s, 1, pTs[s], 'pTs%d' % s, lambda kc, ho=ho: ho[:, kc, :], hk)
                    dma('sp', h2T_d[blk], ho[:].rearrange("p a b -> p (a b)"), reads=[hk], writes=['h2T_d%d' % blk])
            T = dict(k.capture); k.capture = None; k.cur = None
            order = []
            if NG:
                order += [('L', 0), ('Fa', 0), ('Fb', 0)]
                if NG > 1:
                    order += [('L', 1)]
            for g in range(NG):
                order.append(('GU1', g))
                if g + 1 < NG:
                    order.append(('Fa', g + 1))
                order.append(('GU2', g))
                if g >= 1:
                    order.append(('Nb', g - 1))
                if g + 1 < NG:
                    order.append(('Fb', g + 1))
                order += [('D', g), ('Na', g)]
                if g + 2 < NG:
                    order.append(('L', g + 2))
            if NG:
                order.append(('Nb', NG - 1))
            assert sorted(order) == sorted(T.keys()), (len(order), len(T))
            for key_ in order:
                k.replay(T[key_])
            k.es = es0
        k.barrier()
        if stop_after == 1:
            dbg['x1_d'] = (x1_d, [NOWN, 128, D], F32)
            dbg['h2T_d'] = (h2T_d, [NB, 128, 1024], BF16)

        SCALE_Q = 192.0 ** -0.5
        with ExitStack() as p2:
            es0 = k.es; k.es = p2
            Win = sb("Win", [128, 8, NIN], BF16)
            Wuq = sb("Wuq", [128, 3, 768], BF16); Wukv = sb("Wukv", [128, 2, 1024], BF16)
            load_w_bf16(Win, 'Win', w_in, 8, NIN, piece=920)
            load_w_bf16(Wuq, 'Wuq', w_uq, 3, 768)
            load_w_bf16(Wukv, 'Wukv', w_ukv, 2, 1024)

            def cload(name, src, shape, dt=F32):
                t = sb(name, shape, dt)
                dma('sp', t[:], src, writes=[name])
                return t
            utri = cload("utri", c_utri, [128, 128]); maskU = cload("maskU", c_maskU, [128, 128])
            maskLn = cload("maskLn", c_maskL, [128, 128]); bones = cload("bones", c_bones, [128, 128])
            cind = cload("cind", c_cind, [128, 256]); invf = cload("invf", c_invf, [128, 32])
            convc = cload("convc", conv_c, [128, 12, 4]); alog = cload("alog", alog_bc, [128, 4]); dtb = cload("dtb", dtb_bc, [128, 4])
            gnw = cload("gnw", gnw_bc, [128, 512]); qnw = cload("qnw", qnw_col, [128, 3]); kvnw = cload("kvnw", kvnw_col, [128, 2])
            gqn = cload("gqn", gqn_col, [128, 1]); gkn = cload("gkn", gkn_col, [128, 1])
            gqr = cload("gqr", gqr_bc, [128, 64]); gkr = cload("gkr", gkr_bc, [128, 64])
            posi = cload("posi", pos_col, [128, NB], I32)
            posf = sb("posf", [128, NB], F32)
            negA = sb("negA", [128, 4], F32)
            I('act', 'activation', negA[:], alog[:], AF.Exp, reads=['alog'], writes=['negA'])
            I('dve', 'tensor_scalar', negA[:], negA[:], -1.0, None, ALU.mult, reads=['negA'], writes=['negA'])
            I('dve', 'tensor_scalar', gqn[:], gqn[:], SCALE_Q, None, ALU.mult, reads=['gqn'], writes=['gqn'])
            I('dve', 'tensor_copy', posf[:], posi[:], reads=['posi'], writes=['posf'])
            cosT = sb("cosT", [128, NB, 32], F32); sinT = sb("sinT", [128, NB, 32], F32)
            with ExitStack() as pr:
                k.es = pr
                ang = sb("ang", [128, NB, 32], F32); angi = sb("angi", [128, NB, 32], I32)
                fr = sb("fr", [128, NB, 32], F32); gt = sb("gt", [128, NB, 32], F32)
                for b in range(NB):
                    I('dve', 'tensor_scalar', ang[:, b, :], invf[:], posf[:, b:b + 1], None, ALU.mult, reads=['invf', 'posf'], writes=['ang'])
                I('dve', 'tensor_copy', angi[:], ang[:], reads=['ang'], writes=['angi'])
                I('dve', 'tensor_copy', fr[:], angi[:], reads=['angi'], writes=['fr'])
                I('dve', 'tensor_tensor', fr[:], ang[:], fr[:], ALU.subtract, reads=['ang', 'fr'], writes=['fr'])
                for (dst, dk_, add) in ((sinT, 'sinT', 0.0), (cosT, 'cosT', 0.25)):
                    I('dve', 'tensor_scalar', ang[:], fr[:], add, None, ALU.add, reads=['fr'], writes=['ang'])
                    I('dve', 'tensor_scalar', gt[:], ang[:], 0.5, None, ALU.is_gt, reads=['ang'], writes=['gt'])
                    I('dve', 'tensor_tensor', ang[:], ang[:], gt[:], ALU.subtract, reads=['gt', 'ang'], writes=['ang'])
                    I('dve', 'tensor_scalar', gt[:], ang[:], -0.5, None, ALU.is_lt, reads=['ang'], writes=['gt'])
                    I('dve', 'tensor_tensor', ang[:], ang[:], gt[:], ALU.add, reads=['gt', 'ang'], writes=['ang'])
                    I('act', 'activation', dst[:], ang[:], AF.Sin, scale=TWO_PI, reads=['ang'], writes=[dk_])
                k.es = p2
            k.barrier()
            h2 = [sb("h2_%d" % i, [128, 8, 128], BF16) for i in range(2)]
            xbuf = sb("xbuf", [128, 12, 132], BF16); csP = [sb("cs_%d" % i, [128, 8, 128], F32) for i in range(2)]
            diagw = sb("diagw", [128, 12, 4, 128], BF16)
            for ch_ in range(12):
                for tp_ in range(4):
                    I('dve', 'tensor_scalar', diagw[:, ch_, tp_, :], ident_f[:], convc[:, ch_, tp_:tp_ + 1], None, ALU.mult,
                      reads=['ident_f', 'convc'], writes=['diagw'])
            I('dve', 'memset', xbuf[:], 0.0, writes=['xbuf'])
            HB = {}
            for h in range(4):
                for nm in ('gbh', 'gbl', 'vb', 'kbg', 'kdec'):
                    HB[nm, h] = sb("%s_%d" % (nm, h), [128, 128], BF16)
            B4 = {}
            for nm, dt_ in (('sq4q', BF16), ('sq4k', BF16), ('qnT', BF16), ('knT', BF16), ('qdT0', BF16), ('qdT1', BF16),
                            ('rn4q', F32), ('rn4k', F32), ('tU', F32), ('tL', F32), ('expGb', F32), ('E', F32), ('E2', F32)):
                B4[nm] = sb("b4_" + nm, [128, 4, 128], dt_)
            for nm, dt_ in (('X0', F32), ('X1', F32), ('XT0', F32), ('XT1', F32), ('TT', F32), ('TTb', BF16), ('u', F32), ('QKT', BF16),
                            ('wkT0', BF16), ('wkT1', BF16), ('vnew', BF16)):
                B4[nm] = sb("b4_" + nm, [128, 4, 128], dt_)
            for nm in ('qnT', 'knT', 'qdT0', 'qdT1', 'E', 'E2', 'X0', 'X1', 'XT0', 'XT1', 'TT', 'TTb', 'u', 'QKT', 'wkT0', 'wkT1', 'vnew'):
                for h in range(4):
                    HB[nm, h] = B4[nm][:, h, :]
            I('dve', 'memset', B4['qdT0'][:], 0.0, writes=['qdT0_%d' % h for h in range(4)])
            I('dve', 'memset', B4['qdT1'][:], 0.0, writes=['qdT1_%d' % h for h in range(4)])
            I('dve', 'memset', B4['wkT0'][:], 0.0, writes=['wkT0_%d' % h for h in range(4)])
            I('dve', 'memset', B4['wkT1'][:], 0.0, writes=['wkT1_%d' % h for h in range(4)])
            maskU4 = sb("maskU4", [128, 4, 128], F32); maskLn4 = sb("maskLn4", [128, 4, 128], F32); ident4 = sb("ident4", [128, 4, 128], F32)
            for h in range(4):
                dma('sp', maskU4[:, h, :], c_maskU, writes=['maskU4']); dma('sp', maskLn4[:, h, :], c_maskL, writes=['maskLn4'])
                dma('sp', ident4[:, h, :], c_ident, writes=['ident4'])
            S = sb("S", [128, 4, 128], F32); Sb = sb("Sb", [128, 4, 128], BF16)
            I('dve', 'memset', S[:], 0.0, writes=['S_%d' % h for h in range(4)])
            I('dve', 'memset', Sb[:], 0.0, writes=['Sb_%d' % h for h in range(4)])
            ones_b = sb("ones_b", [128, 128], BF16); csvP = [sb("csv_%d" % i, [128, 4, 128], BF16) for i in range(2)]; ghfP = [sb("ghf_%d" % i, [128, 8], F32) for i in range(2)]
            I('dve', 'memset', ones_b[:], 1.0, writes=['ones_b'])
            gstP = [sb("gst_%d" % i, [128, 48], F32) for i in range(2)]
            mlainP = [sb("mlain_%d" % i, [128, 704], F32) for i in range(2)]
            ghl = sb("ghl", [128, 2, 32], BF16)
            I('dve', 'memset', ghl[:], 0.0, writes=['ghl'])
            utri_b = sb("utri_b", [128, 128], BF16); bones_b = sb("bones_b", [128, 128], BF16); cind_b = sb("cind_b", [128, 256], BF16)
            dma('pool', utri_b[:], c_utri, writes=['utri_b']); dma('pool', bones_b[:], c_bones, writes=['bones_b']); dma('pool', cind_b[:], c_cind, writes=['cind_b'])
            zwP = [sb("zw_%d" % i, [128, 512], F32) for i in range(2)]; oa = sb("oa", [128, 512], BF16); osb = sb("osb", [128, 512], F32)
            mtmp = sb("mtmp", [128, 512], F32); mb16 = sb("mb16", [128, 512], BF16)
            cqnT = sb("cqnT", [128, 3, 128], BF16); ckvnT = sb("ckvnT", [128, 2, 128], BF16)
            KTt = sb("KTt", [128, 4, 128], BF16); Vt = sb("Vt", [128, 4, 130], BF16); krTt = sb("krTt", [64, 128], BF16)
            QTnt = sb("QTnt", [128, 4, 128], BF16); QTrt = sb("QTrt", [64, 4, 128], BF16)
            rop = sb("rop", [128, 6, 32], F32); rin = sb("rin", [128, 64], F32); rout = sb("rout", [128, 4, 64], BF16)
            mst = sb("mst", [128, 32], F32); junk2 = sb("junk2", [128, 512], F32)
            I('dve', 'memset', Vt[:], 1.0, writes=['Vt'])
            pp = [ps("pp%d" % i, [128, 512], F32) for i in range(7)]
            ptb = ps("ptb", [128, 8, 128], BF16)
            pgd = pp[0]

            def gslot(h, i):
                return pp[4 + i][:, h * 128:(h + 1) * 128], 'gs%d' % i
            rot = [0] * 4

            def nslot(h):
                rot[h] ^= 1
                return gslot(h, rot[h])

            def hb(nm, h):
                return HB[nm, h], '%s_%d' % (nm, h)

            blocktasks = []
            for blk in range(NB if stop_after >= 2 else 0):
                k.muted = False
                par = blk % 2
                cs, csv, gst, ghf, zw, mlain = csP[par], csvP[par], gstP[par], ghfP[par], zwP[par], mlainP[par]
                km = {'gst': 'gst_%d' % par, 'ghf': 'ghf_%d' % par, 'zw': 'zw_%d' % par, 'mlain': 'mlain_%d' % par}
                km.update({'cs%d' % i: 'cs%d_%d' % (i, par) for i in range(12)})
                km.update({'csv%d' % i: 'csv%d_%d' % (i, par) for i in range(4)})
                k.keymap = km
                k.capture_start(); k.task('p')
                own = (blk % 2 == 0)
                oi = blk // 2
                ht, hk = h2[blk % 2], 'h2_%d' % (blk % 2)
                dma('sp', ht[:].rearrange("p a b -> p (a b)"), h2T_d[blk], reads=['h2T_d%d' % blk], writes=[hk])
                for kc in range(8):
                    I('pe', 'matmul', pp[2][:, 0:392], ht[:, kc, :], Win[:, kc, OFF_A:OFF_A + 392], start=(kc == 0), stop=(kc == 7), reads=[hk, 'Win'], writes=['pp2'])
                for kc in range(8):
                    I('pe', 'matmul', pp[3][:, 0:320], ht[:, kc, :], Win[:, kc, OFF_CKV:OFF_CKV + 320], start=(kc == 0), stop=(kc == 7), reads=[hk, 'Win'], writes=['pp3'])
                I('act', 'activation', mlain[:, 0:384], pp[2][:, 8:392], AF.Identity, reads=['pp2'], writes=['mlain'])
                I('act', 'activation', mlain[:, 384:704], pp[3][:, 0:320], AF.Identity, reads=['pp3'], writes=['mlain'])
                I('dve', 'tensor_tensor', gst[:, 8:12], pp[2][:, 0:4], dtb[:], ALU.add, reads=['pp2', 'dtb'], writes=['gst'])
                I('act', 'activation', gst[:, 8:12], gst[:, 8:12], AF.Exp, reads=['gst'], writes=['gst'])
                I('act', 'activation', gst[:, 8:12], gst[:, 8:12], AF.Ln, bias=1.0, reads=['gst'], writes=['gst'])
                I('dve', 'tensor_tensor', gst[:, 0:4], gst[:, 8:12], negA[:], ALU.mult, reads=['gst', 'negA'], writes=['gst'])
                I('act', 'activation', gst[:, 12:16], pp[2][:, 4:8], AF.Exp, scale=-1.0, reads=['pp2'], writes=['gst'])
                I('dve', 'tensor_scalar', gst[:, 12:16], gst[:, 12:16], 1.0, None, ALU.add, reads=['gst'], writes=['gst'])
                I('dve', 'reciprocal', gst[:, 4:8], gst[:, 12:16], reads=['gst'], writes=['gst'])
                if blk == 0:
                    I('dve', 'tensor_scalar', gst[:, 4:8], gst[:, 4:8], flg[:, 0:1], None, ALU.mult, reads=['gst', 'flg'], writes=['gst'])
                I('dve', 'tensor_copy', ghl[:, 0, 0:4], gst[:, 0:4], reads=['gst'], writes=['ghl'])
                I('dve', 'tensor_tensor', ghl[:, 1, 0:4], gst[:, 0:4], ghl[:, 0, 0:4], ALU.subtract, reads=['gst', 'ghl'], writes=['ghl'])
                I('dve', 'tensor_copy', ghf[:, 0:4], ghl[:, 0, 0:4], reads=['ghl'], writes=['ghf'])
                I('dve', 'tensor_tensor', ghf[:, 4:8], gst[:, 0:4], ghf[:, 0:4], ALU.subtract, reads=['gst', 'ghf'], writes=['ghf'])
                for (cols, lt, ltk) in (((320, 352), utri_b[:], 'utri_b'), ((352, 384), bones_b[:], 'bones_b'),
                                        ((384, 416), cind_b[:, 0:128], 'cind_b'), ((416, 448), cind_b[:, 128:256], 'cind_b')):
                    for part in range(2):
                        I('pe', 'matmul', pgd[:, cols[0]:cols[1]], lt, ghl[:, part, :], start=(part == 0), stop=(part == 1), reads=[ltk, 'ghl'], writes=['pp0_2'])
                I('dve', 'tensor_copy', gst[:, 16:20], pgd[:, 320:324], reads=['pp0_2'], writes=['gst'])
                I('dve', 'tensor_scalar', gst[:, 20:24], pgd[:, 320:324], -1.0, None, ALU.mult, reads=['pp0_2'], writes=['gst'])
                I('act', 'activation', gst[:, 24:28], gst[:, 16:20], AF.Exp, reads=['gst'], writes=['gst'])
                I('dve', 'tensor_tensor', gst[:, 24:28], gst[:, 24:28], gst[:, 4:8], ALU.mult, reads=['gst'], writes=['gst'])
                I('dve', 'tensor_tensor', gst[:, 28:32], pgd[:, 352:356], gst[:, 16:20], ALU.subtract, reads=['pp0_2', 'gst'], writes=['gst'])
                I('act', 'activation', gst[:, 28:32], gst[:, 28:32], AF.Exp, reads=['gst'], writes=['gst'])
                for c in range(2):
                    I('act', 'activation', gst[:, 32 + 4 * c:36 + 4 * c], pgd[:, 384 + 32 * c:388 + 32 * c], AF.Exp, reads=['pp0_2'], writes=['gst'])
                def conv_proj(ch):
                    pq = (pp[0] if ch % 2 == 0 else pp[2])[:, 0:128]; pqk = ('pp0_2' if ch % 2 == 0 else 'pp2')
                    for kc in range(8):
                        I('pe', 'matmul', pq, Win[:, kc, ch * 128:(ch + 1) * 128], ht[:, kc, :], start=(kc == 0), stop=(kc == 7), reads=[hk, 'Win'], writes=[pqk])
                    xk = 'xbuf%d' % ch
                    if blk == 0:
                        I('act', 'activation', xbuf[:, ch, 3:131], pq, AF.Identity, scale=flg[:, 0:1], reads=[pqk, 'flg', 'xbuf'], writes=[xk])
                    else:
                        I('act', 'activation', xbuf[:, ch, 3:131], pq, AF.Identity, reads=[pqk, 'xbuf'], writes=[xk])

                def conv_taps(ch):
                    xk = 'xbuf%d' % ch
                    pc = pp[3][:, 0:128]
                    for tp in range(4):
                        I('pe', 'matmul', pc, diagw[:, ch, tp, :], xbuf[:, ch, tp:tp + 128], start=(tp == 0), stop=(tp == 3),
                          reads=['diagw', xk], writes=['pp3'])
                    I('pool', 'tensor_copy', xbuf[:, ch, 0:3], xbuf[:, ch, 128:131], reads=[xk], writes=[xk])
                    if ch < 8:
                        I('act', 'activation', cs[:, ch, :], pc, AF.Silu, reads=['pp3'], writes=['cs%d' % ch])
                    else:
                        I('act', 'activation', csv[:, ch - 8, :], pc, AF.Silu, reads=['pp3'], writes=['csv%d' % (ch - 8)])

                conv_proj(0)
                for ch in range(12):
                    if ch + 1 < 12:
                        conv_proj(ch + 1)
                    conv_taps(ch)
                if own:
                    for kc in range(8):
                        I('pe', 'matmul', pp[3][:, :], ht[:, kc, :], Win[:, kc, OFF_Z:OFF_Z + 512], start=(kc == 0), stop=(kc == 7), reads=[hk, 'Win'], writes=['pp3'])
                    I('act', 'activation', zw[:], pp[3][:, :], AF.Silu, reads=['pp3'], writes=['zw'])
                    I('dve', 'tensor_tensor', zw[:], zw[:], gnw[:], ALU.mult, reads=['zw', 'gnw'], writes=['zw'])
                k.task('g')
                H4 = range(4)
                for (c0, sqn, bank, bankk, rnn, outn, scl) in ((0, 'sq4q', pp[4], 'gs0', 'rn4q', 'qnT', 128.0 ** -0.5),
                                                                (4, 'sq4k', pp[5], 'gs1', 'rn4k', 'knT', 1.0)):
                    sq, rn, o4 = B4[sqn], B4[rnn], B4[outn]
                    csk = ['cs%d' % (c0 + i) for i in range(4)]
                    csf = cs[:, c0:c0 + 4, :].rearrange("p a b -> p (a b)")
                    I('act', 'activation', sq[:].rearrange("p a b -> p (a b)"), csf, AF.Square, reads=csk, writes=[sqn])
                    I('pe', 'matmul', bank[:, :], ones_b[:], sq[:].rearrange("p a b -> p (a b)"), start=True, stop=True, reads=['ones_b', sqn], writes=[bankk])
                    I('act', 'activation', rn[:].rearrange("p a b -> p (a b)"), bank[:, :], AF.Ln, bias=eps_col[:, 0:1], reads=[bankk, 'eps_col'], writes=[rnn])
                    I('act', 'activation', rn[:].rearrange("p a b -> p (a b)"), rn[:].rearrange("p a b -> p (a b)"), AF.Exp, scale=-0.5, reads=[rnn], writes=[rnn])
                    I('dve', 'scalar_tensor_tensor', o4[:].rearrange("p a b -> p (a b)"), csf, scl, rn[:].rearrange("p a b -> p (a b)"), ALU.mult, ALU.mult,
                      reads=csk + [rnn], writes=['%s_%d' % (outn, h) for h in range(4)])
                k.task('g')
                slots = [nslot(h) for h in H4]
                gbank = pp[4 + rot[0]]; gbankk = 'gs%d' % rot[0]
                for h in H4:
                    gbh, gbhk = hb('gbh', h); gbl, gblk = hb('gbl', h)
                    I('dve', 'tensor_scalar', gbh[:], ones_f[:], ghf[:, h:h + 1], None, ALU.mult, reads=['ones_f', 'ghf'], writes=[gbhk])
                    I('dve', 'tensor_scalar', gbl[:], ones_f[:], ghf[:, 4 + h:5 + h], None, ALU.mult, reads=['ones_f', 'ghf'], writes=[gblk])
                    sl, slk = slots[h]
                    I('pe', 'matmul', sl, gbh[:], utri_b[:], start=True, stop=False, reads=[gbhk, 'utri_b'], writes=[slk])
                    I('pe', 'matmul', sl, gbl[:], utri_b[:], start=False, stop=True, reads=[gblk, 'utri_b'], writes=[slk])
                fl = lambda t: t[:].rearrange("p a b -> p (a b)")
                I('dve', 'tensor_tensor', fl(B4['tU']), gbank[:, :], fl(maskU4), ALU.add, reads=[gbankk, 'maskU4'], writes=['tU4'])
                I('dve', 'tensor_tensor', fl(B4['tL']), gbank[:, :], fl(maskLn4), ALU.add, reads=[gbankk, 'maskLn4'], writes=['tL4'])
                I('act', 'activation', fl(B4['expGb']), gbank[:, :], AF.Exp, reads=[gbankk], writes=['eg4'])
                for h in H4:
                    E, Ek = hb('E', h); E2, E2k = hb('E2', h)
                    I('act', 'activation', E[:], B4['tU'][:, h, :], AF.Exp, bias=gst[:, 20 + h:21 + h], reads=['tU4', 'gst'], writes=[Ek])
                    I('act', 'activation', E2[:], B4['tL'][:, h, :], AF.Exp, scale=-1.0, bias=gst[:, 16 + h:17 + h], reads=['tL4', 'gst'], writes=[E2k])
                E2ks = ['E2_%d' % h for h in H4]
                I('dve', 'tensor_tensor', fl(B4['E2']), fl(B4['E2']), fl(ident4), ALU.subtract, reads=E2ks + ['ident4'], writes=E2ks)
                for c in range(2):
                    I('dve', 'tensor_tensor', B4['qdT%d' % c][:, :, c * 64:(c + 1) * 64], B4['qnT'][:, :, c * 64:(c + 1) * 64],
                      B4['expGb'][:, :, c * 64:(c + 1) * 64], ALU.mult, reads=['qnT_%d' % h for h in H4] + ['eg4'],
                      writes=['qdT%d_%d' % (c, h) for h in H4])
                k.task('g')
                K4 = lambda nm: ['%s_%d' % (nm, h) for h in H4]
                slA = [nslot(h) for h in H4]
                for h in H4:
                    kn, knk = hb('knT', h)
                    I('pe', 'matmul', slA[h][0], kn[:], kn[:], start=True, stop=True, reads=[knk], writes=[slA[h][1]])
                for h in H4:
                    E2, E2k = hb('E2', h); X0, X0k = hb('X0', h)
                    I('dve', 'scalar_tensor_tensor', X0[:], slA[h][0], gst[:, 4 + h:5 + h], E2[:], ALU.mult, ALU.mult, reads=[slA[h][1], 'gst', E2k], writes=[X0k])
                slB = [nslot(h) for h in H4]
                bkB = pp[4 + rot[0]]; bkBk = 'gs%d' % rot[0]
                for h in H4:
                    kn, knk = hb('knT', h); qn, qnk = hb('qnT', h)
                    I('pe', 'matmul', slB[h][0], kn[:], qn[:], start=True, stop=True, reads=[knk, qnk], writes=[slB[h][1]])
                I('dve', 'tensor_tensor', fl(B4['QKT']), bkB[:, :], fl(B4['E']), ALU.mult, reads=[bkBk] + K4('E'), writes=K4('QKT'))
                k.task('g')
                for h in H4:
                    X0, X0k = hb('X0', h)
                    sl3, sl3k = nslot(h)
                    I('pe', 'transpose', sl3, X0[:], ident_f[:], reads=[X0k, 'ident_f'], writes=[sl3k])
                bk = pp[4 + rot[0]]; bkk = 'gs%d' % rot[0]
                I('act', 'activation', fl(B4['XT0']), bk[:, :], AF.Identity, reads=[bkk], writes=K4('XT0'))
                I('dve', 'tensor_tensor', fl(B4['TT']), fl(ident4), bk[:, :], ALU.subtract, reads=[bkk, 'ident4'], writes=K4('TT'))
                cur = 0
                for lvl in range(1, 6):
                    k.task('g')
                    nxt = cur ^ 1
                    for h in H4:
                        X, Xk = hb('X%d' % cur, h); XT, XTk = hb('XT%d' % cur, h)
                        sl, slk = nslot(h)
                        I('pe', 'matmul', sl, XT[:], X[:], start=True, stop=True, reads=[XTk, Xk], writes=[slk])
                    bk = pp[4 + rot[0]]; bkk = 'gs%d' % rot[0]
                    I('act', 'activation', fl(B4['X%d' % nxt]), bk[:, :], AF.Identity, reads=[bkk], writes=K4('X%d' % nxt))
                    if lvl < 5:
                        for h in H4:
                            X, Xk = hb('X%d' % cur, h); XT, XTk = hb('XT%d' % cur, h)
                            sl2, sl2k = nslot(h)
                            I('pe', 'matmul', sl2, X[:], XT[:], start=True, stop=True, reads=[XTk, Xk], writes=[sl2k])
                        bk = pp[4 + rot[0]]; bkk = 'gs%d' % rot[0]
                        I('dve', 'tensor_copy', fl(B4['XT%d' % nxt]), bk[:, :], reads=[bkk], writes=K4('XT%d' % nxt))
                    for h in H4:
                        Xn, Xnk = hb('X%d' % nxt, h); TT, TTk = hb('TT', h)
                        sl, slk = nslot(h)
                        I('pe', 'matmul', sl, Xn[:], TT[:], start=True, stop=True, reads=[Xnk, TTk], writes=[slk])
                    bk = pp[4 + rot[0]]; bkk = 'gs%d' % rot[0]
                    I('dve', 'tensor_tensor', fl(B4['TT']), fl(B4['TT']), bk[:, :], ALU.add, reads=[bkk] + K4('TT'), writes=K4('TT'))
                    cur = nxt
                k.task('g')
                I('act', 'activation', fl(B4['TTb']), fl(B4['TT']), AF.Identity, reads=K4('TT'), writes=K4('TTb'))
                for h in H4:
                    kn, knk = hb('knT', h)
                    I('pe', 'transpose', ptb[:, h, :], kn[:], ident_b[:], reads=[knk, 'ident_b'], writes=['ptbG'])
                for h in H4:
                    kbg, kbgk = hb('kbg', h)
                    I('act', 'activation', kbg[:], ptb[:, h, :], AF.Identity, scale=gst[:, 24 + h:25 + h], reads=['ptbG', 'gst'], writes=[kbgk])
                for h in H4:
                    kdec, kdk = hb('kdec', h)
                    I('dve', 'tensor_scalar', kdec[:], ptb[:, h, :], gst[:, 28 + h:29 + h], None, ALU.mult, reads=['ptbG', 'gst'], writes=[kdk])
                for h in H4:
                    I('pe', 'transpose', ptb[:, h, :], csv[:, h, :], ident_b[:], reads=['csv%d' % h, 'ident_b'], writes=['ptbG'])
                for h in H4:
                    vb, vbk = hb('vb', h)
                    I('act', 'activation', vb[:], ptb[:, h, :], AF.Identity, scale=gst[:, 4 + h:5 + h], reads=['ptbG', 'gst'], writes=[vbk])
                k.task('g')
                slU = [nslot(h) for h in H4]
                bkU = pp[4 + rot[0]]; bkUk = 'gs%d' % rot[0]
                for h in H4:
                    vb, vbk = hb('vb', h); TT, TTk = hb('TTb', h)
                    I('pe', 'matmul', slU[h][0], TT[:], vb[:], start=True, stop=True, reads=[TTk, vbk], writes=[slU[h][1]])
                I('act', 'activation', fl(B4['u']), bkU[:, :], AF.Identity, reads=[bkUk], writes=K4('u'))
                slW = [nslot(h) for h in H4]
                bkW = pp[4 + rot[0]]; bkWk = 'gs%d' % rot[0]
                for h in H4:
                    kbg, kbgk = hb('kbg', h); TT, TTk = hb('TTb', h)
                    I('pe', 'matmul', slW[h][0], kbg[:], TT[:], start=True, stop=True, reads=[kbgk, TTk], writes=[slW[h][1]])
                bkW3 = bkW[:, :].rearrange("p (a b) -> p a b", a=4)
                for c in range(2):
                    I('dve', 'tensor_copy', B4['wkT%d' % c][:, :, c * 64:(c + 1) * 64], bkW3[:, :, c * 64:(c + 1) * 64], reads=[bkWk], writes=K4('wkT%d' % c))
                for c in range(2):
                    k.task('g')
                    r0, r1 = c * 64, (c + 1) * 64
                    slV = [nslot(h) for h in H4]
                    bkV = pp[4 + rot[0]]; bkVk = 'gs%d' % rot[0]
                    for h in H4:
                        wk, wkk = hb('wkT%d' % c, h)
                        I('pe', 'matmul', slV[h][0], wk[:], Sb[:, h, :], start=True, stop=True, reads=[wkk, 'Sb_%d' % h], writes=[slV[h][1]])
                    I('dve', 'tensor_tensor', fl(B4['vnew'])[r0:r1, :], fl(B4['u'])[r0:r1, :], bkV[r0:r1, :], ALU.subtract, reads=K4('u') + [bkVk], writes=K4('vnew'))
                    if own:
                        for h in H4:
                            qd, qdk = hb('qdT%d' % c, h); QKT, QKTk = hb('QKT', h); vn, vnk = hb('vnew', h)
                            po = pp[6][:, h * 128:(h + 1) * 128]
                            I('pe', 'matmul', po, qd[:], Sb[:, h, :], start=True, stop=False, reads=[qdk, 'Sb_%d' % h], writes=['po_all'])
                            I('pe', 'matmul', po, QKT[r0:r1, :], vn[r0:r1, :], start=False, stop=True, reads=[QKTk, vnk], writes=['po_all'])
                        I('act', 'activation', osb[r0:r1, :], pp[6][r0:r1, :], AF.Identity, reads=['po_all'], writes=['osb'])
                    slS = [nslot(h) for h in H4]
                    for h in H4:
                        vn, vnk = hb('vnew', h); kdec, kdk = hb('kdec', h)
                        I('pe', 'matmul', slS[h][0], kdec[r0:r1, :], vn[r0:r1, :], start=True, stop=True, reads=[kdk, vnk], writes=[slS[h][1]])
                    for h in H4:
                        I('dve', 'scalar_tensor_tensor', S[:, h, :], S[:, h, :], gst[:, 32 + 4 * c + h:33 + 4 * c + h], slS[h][0], ALU.mult, ALU.add,
                          reads=[slS[h][1], 'gst', 'S_%d' % h], writes=['S_%d' % h])
                    I('act', 'activation', fl(Sb), fl(S), AF.Identity, reads=['S_%d' % h for h in H4], writes=['Sb_%d' % h for h in H4])
                k.task('g')
                if own:
                    for h in H4:
                        I('act', 'activation', junk2[:, 0:128], osb[:, h * 128:(h + 1) * 128], AF.Square, accum_out=gst[:, 40 + h:41 + h],
                          reads=['osb'], writes=['junk2', 'gst'])
                    rstd_from_ss(gst[:, 40:44], gst[:, 44:48], 128, ['gst'], ['gst'], gst[:, 40:44])
                    for h in H4:
                        I('dve', 'scalar_tensor_tensor', oa[:, h * 128:(h + 1) * 128], osb[:, h * 128:(h + 1) * 128], gst[:, 44 + h:45 + h],
                          zw[:, h * 128:(h + 1) * 128], ALU.mult, ALU.mult, reads=['osb', 'gst', 'zw'], writes=['oa'])
                    dma('sp', oa_d[oi], oa[:], reads=['oa'], writes=['oa_d%d' % oi])

                def small_rms(src_ap, n, col, srck):
                    I('act', 'activation', junk2[:, 0:n], src_ap, AF.Square, accum_out=mst[:, col:col + 1], reads=[srck], writes=['junk2', 'mst'])
                    rstd_from_ss(mst[:, col:col + 1], mst[:, col + 1:col + 2], n, ['mst'], ['mst'], mst[:, col:col + 1])
                    return mst[:, col + 1:col + 2]

                def rope_T(src_ap, srck, rstd_ap, gbc, gbck, out_ap, outk, mult):
                    I('dve', 'scalar_tensor_tensor', rin[:], src_ap, rstd_ap, gbc[:], ALU.mult, ALU.mult, reads=[srck, 'mst', gbck], writes=['rin'])
                    x1, x2 = rin[:, 0:32], rin[:, 32:64]
                    cb, sb_ = cosT[:, blk, :], sinT[:, blk, :]
                    I('pool', 'tensor_tensor', rop[:, 0, :], x1, cb, ALU.mult, reads=['rin', 'cosT'], writes=['rop'])
                    I('pool', 'tensor_tensor', rop[:, 1, :], x2, sb_, ALU.mult, reads=['rin', 'sinT'], writes=['rop'])
                    I('pool', 'tensor_tensor', rop[:, 2, :], x2, cb, ALU.mult, reads=['rin', 'cosT'], writes=['rop'])
                    I('pool', 'tensor_tensor', rop[:, 3, :], x1, sb_, ALU.mult, reads=['rin', 'sinT'], writes=['rop'])
                    I('pool', 'tensor_tensor', rop[:, 4, :], rop[:, 0, :], rop[:, 1, :], ALU.subtract, reads=['rop'], writes=['rop'])
                    I('pool', 'tensor_tensor', rop[:, 5, :], rop[:, 2, :], rop[:, 3, :], ALU.add, reads=['rop'], writes=['rop'])
                    I('dve', 'tensor_scalar', out_ap, rop[:, 4:6, :].rearrange("p a b -> p (a b)"), mult, None, ALU.mult, reads=['rop'], writes=[outk])

                k.task('m')
                r_ = small_rms(mlain[:, 384:640], 256, 0, 'mlain')
                I('act', 'activation', mb16[:, 0:256], mlain[:, 384:640], AF.Identity, scale=r_, reads=['mlain', 'mst'], writes=['mb16'])
                for kc in range(2):
                    I('pe', 'transpose', ptb[:, 4 + kc, :], mb16[:, kc * 128:(kc + 1) * 128], ident_b[:], reads=['mb16', 'ident_b'], writes=['ptbM'])
                for kc in range(2):
                    I('act', 'activation', ckvnT[:, kc, :], ptb[:, 4 + kc, :], AF.Identity, scale=kvnw[:, kc:kc + 1], reads=['ptbM', 'kvnw'], writes=['ckvnT'])
                k.task('m')
                r_ = small_rms(mlain[:, 640:704], 64, 2, 'mlain')
                rope_T(mlain[:, 640:704], 'mlain', r_, gkr, 'gkr', rout[:, 0, :], 'rout', 1.0)
                I('pe', 'transpose', ptb[0:64, 6, :], rout[:, 0, :], ident_b[:], reads=['rout', 'ident_b'], writes=['ptbM'])
                I('act', 'activation', krTt[:], ptb[0:64, 6, :], AF.Identity, reads=['ptbM'], writes=['krTt'])
                dma('sp', krT_d[blk], krTt[:], reads=['krTt'], writes=['krT_d%d' % blk])
                k.task('m')
                for half in range(2):
                    for kc in range(2):
                        I('pe', 'matmul', pp[1][:, :], ckvnT[:, kc, :], Wukv[:, kc, half * 512:(half + 1) * 512], start=(kc == 0), stop=(kc == 1),
                          reads=['ckvnT', 'Wukv'], writes=['pp1'])
                    for hh in range(2):
                        h = half * 2 + hh
                        kap = pp[1][:, hh * 256:hh * 256 + 128]
                        r_ = small_rms(kap, 128, 4 + 2 * hh, 'pp1')
                        I('act', 'activation', mb16[:, hh * 128:(hh + 1) * 128], kap, AF.Identity, scale=r_, reads=['pp1', 'mst'], writes=['mb16'])
                        I('pe', 'transpose', ptb[:, 4 + hh, :], mb16[:, hh * 128:(hh + 1) * 128], ident_b[:], reads=['mb16', 'ident_b'], writes=['ptbM'])
                        I('act', 'activation', KTt[:, h, :], ptb[:, 4 + hh, :], AF.Identity, scale=gkn[:, 0:1], reads=['ptbM', 'gkn'], writes=['KTt'])
                        I('act', 'activation', Vt[:, h, 0:128], pp[1][:, hh * 256 + 128:hh * 256 + 256], AF.Identity, reads=['pp1'], writes=['Vt'])
                k.task('m')
                dma('sp', KT_d[blk], KTt[:].rearrange("p a b -> p (a b)"), reads=['KTt'], writes=['KT_d%d' % blk])
                dma('sp', V_d[blk], Vt[:].rearrange("p a b -> p (a b)"), reads=['Vt'], writes=['V_d%d' % blk])
                if own:
                    k.task('m')
                    r_ = small_rms(mlain[:, 0:384], 384, 8, 'mlain')
                    I('act', 'activation', mb16[:, 0:384], mlain[:, 0:384], AF.Identity, scale=r_, reads=['mlain', 'mst'], writes=['mb16'])
                    for kc in range(3):
                        I('pe', 'transpose', ptb[:, 4 + kc, :], mb16[:, kc * 128:(kc + 1) * 128], ident_b[:], reads=['mb16', 'ident_b'], writes=['ptbM'])
                    for kc in range(3):
                        I('act', 'activation', cqnT[:, kc, :], ptb[:, 4 + kc, :], AF.Identity, scale=qnw[:, kc:kc + 1], reads=['ptbM', 'qnw'], writes=['cqnT'])
                    k.task('m')
                    for pair in range(2):
                        for kc in range(3):
                            I('pe', 'matmul', pp[1][:, 0:384], cqnT[:, kc, :], Wuq[:, kc, pair * 384:(pair + 1) * 384], start=(kc == 0), stop=(kc == 2),
                              reads=['cqnT', 'Wuq'], writes=['pp1'])
                        for hh in range(2):
                            h = pair * 2 + hh
                            nap = pp[1][:, hh * 192:hh * 192 + 128]; rap = pp[1][:, hh * 192 + 128:hh * 192 + 192]
                            r_ = small_rms(nap, 128, 10 + 4 * hh, 'pp1')
                            I('act', 'activation', mb16[:, hh * 128:(hh + 1) * 128], nap, AF.Identity, scale=r_, reads=['pp1', 'mst'], writes=['mb16'])
                            I('pe', 'transpose', ptb[:, 4 + hh, :], mb16[:, hh * 128:(hh + 1) * 128], ident_b[:], reads=['mb16', 'ident_b'], writes=['ptbM'])
                            I('act', 'activation', QTnt[:, h, :], ptb[:, 4 + hh, :], AF.Identity, scale=gqn[:, 0:1], reads=['ptbM', 'gqn'], writes=['QTnt'])
                            r2_ = small_rms(rap, 64, 12 + 4 * hh, 'pp1')
                            rope_T(rap, 'pp1', r2_, gqr, 'gqr', rout[:, 1 + hh, :], 'rout', SCALE_Q)
                            I('pe', 'transpose', ptb[0:64, 6 + hh, :], rout[:, 1 + hh, :], ident_b[:], reads=['rout', 'ident_b'], writes=['ptbM'])
                            I('act', 'activation', QTrt[:, h, :], ptb[0:64, 6 + hh, :], AF.Identity, reads=['ptbM'], writes=['QTrt'])
                    k.task('m')
                    dma('sp', QTn_d[oi], QTnt[:].rearrange("p a b -> p (a b)"), reads=['QTnt'], writes=['QTn_d%d' % oi])
                    dma('sp', QTr_d[oi], QTrt[:].rearrange("p a b -> p (a b)"), reads=['QTrt'], writes=['QTr_d%d' % oi])
                blocktasks.append(k.capture); k.capture = None; k.cur = None
            k.muted = False
            k.keymap = {}
            ST = [{tag: [c for t_, t in tasks if t_ == tag for c in t] for tag in 'pgm'} for tasks in blocktasks]
            if ST:
                k.replay(ST[0]['p'])
            for b_ in range(len(ST)):
                ss = [x for x in (ST[b_]['g'], ST[b_]['m'], ST[b_ + 1]['p'] if b_ + 1 < len(ST) else []) if x]
                k.replay(KB.merge(ss))
            k.es = es0
        k.barrier()
        if stop_after == 2:
            dbg['oa_d'] = (oa_d, [NOWN, 128, 512], BF16)
            dbg['KT_d'] = (KT_d, [NB, 128, 512], BF16)
            dbg['V_d'] = (V_d, [NB, 128, 520], BF16)
            dbg['krT_d'] = (krT_d, [NB, 64, 128], BF16)
            dbg['QTn_d'] = (QTn_d, [NOWN, 128, 512], BF16)
            dbg['QTr_d'] = (QTr_d, [NOWN, 64, 512], BF16)

        if stop_after >= 3:
          with ExitStack() as p3:
            es0 = k.es; k.es = p3
            KTa = sb("KTa", [128, NB, 512], BF16); Va = sb("Va", [128, NB, 520], BF16); krTa = sb("krTa", [64, NB, 128], BF16)
            allK = ['KT_d%d' % b for b in range(NB)]; allV = ['V_d%d' % b for b in range(NB)]; allR = ['krT_d%d' % b for b in range(NB)]
            for a in range(0, NB, 16):
                b_ = min(NB, a + 16)
                dma('sp', KTa[:, a:b_, :], KT_d[a:b_].rearrange("n p f -> p n f"), reads=allK, writes=['KTa'])
                dma('sp', Va[:, a:b_, :], V_d[a:b_].rearrange("n p f -> p n f"), reads=allV, writes=['Va'])
                dma('sp', krTa[:, a:b_, :], krT_d[a:b_].rearrange("n p f -> p n f"), reads=allR, writes=['krTa'])
            dmask = sb("dmask", [128, 512], F32); onw = sb("onw", [128, 512], F32)
            dma('sp', dmask[:], c_dmask, writes=['dmask']); dma('sp', onw[:], onw_bc, writes=['onw'])
            QTn = [sb("QTn%d" % i, [128, 4, 128], BF16) for i in range(2)]
            QTr = [sb("QTr%d" % i, [64, 4, 128], BF16) for i in range(2)]
            PT = [sb("PT%d" % i, [128, 512], BF16) for i in range(2)]
            otmp = sb("otmp", [128, 512], F32); ob = sb("ob", [128, 512], BF16); ast = sb("ast", [128, 16], F32); junk3 = sb("junk3", [128, 128], F32)
            psS = [ps("psS%d" % i, [128, 512], F32) for i in range(2)]
            po2 = [ps("po2_%d" % i, [128, 512], F32) for i in range(4)]
            cnt = 0
            for oi, j in enumerate(OWN):
                qn_, qnk = QTn[oi % 2], 'QTn%d' % (oi % 2); qr_, qrk = QTr[oi % 2], 'QTr%d' % (oi % 2)
                for o2 in ([0, 1] if oi == 0 else [oi + 1]):
                    if o2 < NOWN:
                        dma('sp', QTn[o2 % 2][:].rearrange("p a b -> p (a b)"), QTn_d[o2], reads=['QTn_d%d' % o2], writes=['QTn%d' % (o2 % 2)])
                        dma('sp', QTr[o2 % 2][:].rearrange("p a b -> p (a b)"), QTr_d[o2], reads=['QTr_d%d' % o2], writes=['QTr%d' % (o2 % 2)])
                def emit_qk(kb, b):
                    pS, pSk = psS[b], 'psS%d' % b
                    I('pe', 'matmul', pS[:, :], krTa[:, kb, :], qr_[:].rearrange("p a b -> p (a b)"), start=True, stop=False, reads=['krTa', qrk], writes=[pSk])
                    for h in range(4):
                        I('pe', 'matmul', pS[:, h * 128:(h + 1) * 128], KTa[:, kb, h * 128:(h + 1) * 128], qn_[:, h, :], start=False, stop=True,
                          reads=['KTa', qnk], writes=[pSk])
                emit_qk(0, cnt % 2)
                for kb in range(j + 1):
                    b = cnt % 2; cnt += 1
                    pS, pSk = psS[b], 'psS%d' % b; pt, ptk = PT[b], 'PT%d' % b
                    if kb + 1 <= j:
                        emit_qk(kb + 1, cnt % 2)
                    if kb == 0:
                        I('act', 'activation', pt[:], pS[:, :], AF.Exp, bias=flg[:, 1:2], reads=[pSk, 'flg'], writes=[ptk])
                    else:
                        I('act', 'activation', pt[:], pS[:, :], AF.Exp, reads=[pSk], writes=[ptk])
                    if kb == j:
                        I('dve', 'tensor_tensor', pt[:], pt[:], dmask[:], ALU.mult, reads=[ptk, 'dmask'], writes=[ptk])
                    for h in range(4):
                        I('pe', 'matmul', po2[h][:, 0:130], pt[:, h * 128:(h + 1) * 128], Va[:, kb, h * 130:(h + 1) * 130],
                          start=(kb == 0), stop=(kb == j), reads=[ptk, 'Va'], writes=['po2_%d' % h])
                for h in range(4):
                    pv_ = po2[h][:, 0:130]
                    I('dve', 'reciprocal', ast[:, h:h + 1], pv_[:, 128:129], reads=['po2_%d' % h], writes=['ast'])
                    I('dve', 'tensor_scalar', otmp[:, h * 128:(h + 1) * 128], pv_[:, 0:128], ast[:, h:h + 1], None, ALU.mult, reads=['po2_%d' % h, 'ast'], writes=['otmp'])
                    I('act', 'activation', junk3[:], otmp[:, h * 128:(h + 1) * 128], AF.Square, accum_out=ast[:, 4 + h:5 + h], reads=['otmp'], writes=['junk3', 'ast'])
                rstd_from_ss(ast[:, 4:8], ast[:, 8:12], 128, ['ast'], ['ast'], ast[:, 4:8])
                for h in range(4):
                    I('dve', 'scalar_tensor_tensor', ob[:, h * 128:(h + 1) * 128], otmp[:, h * 128:(h + 1) * 128], ast[:, 8 + h:9 + h],
                      onw[:, h * 128:(h + 1) * 128], ALU.mult, ALU.mult, reads=['otmp', 'ast', 'onw'], writes=['ob'])
                dma('sp', ob_d[oi], ob[:], reads=['ob'], writes=['ob_d%d' % oi])
            k.es = es0
        k.barrier()
        if stop_after == 3:
            dbg['ob_d'] = (ob_d, [NOWN, 128, 512], BF16)

        if stop_after >= 4:
          with ExitStack() as p4:
            es0 = k.es; k.es = p4
            Wi = sb("Wi2", [128, 8, 2 * DFF], BF16); Wo = sb("Wo2", [128, NFC, D], BF16); Wm = sb("Wm", [128, 8, D], BF16)
            load_w_bf16(Wm, 'Wm', w_o, 8, D)
            load_w_bf16(Wi, 'Wi2', w2i, 8, 2 * DFF)
            load_w_bf16(Wo, 'Wo2', w2o, NFC, D)
            gT = sb("gT4", [128, D], F32)
            dma('sp', gT[:], gB_d[1], reads=['gB_d1'], writes=['gT'])
            for kc in range(8):
                I('dve', 'tensor_tensor', Wm[:, kc, :], Wm[:, kc, :], gT[:], ALU.mult, reads=['Wm', 'gT'], writes=['Wm'])
            dma('sp', gT[:], gB_d[2], reads=['gB_d2'], writes=['gT'])
            for fc in range(NFC):
                I('dve', 'tensor_tensor', Wo[:, fc, :], Wo[:, fc, :], gT[:], ALU.mult, reads=['Wo2', 'gT'], writes=['Wo2'])
            xts = [sb("xg4_%d" % i, [128, G, D], F32) for i in range(2)]; mix = sb("mix", [128, G, D], BF16)
            hTs4 = [sb("hT4_%d" % i, [128, 8, GT_MAX], BF16) for i in range(2)]; aT = sb("aT4", [128, NFC, GT_MAX], BF16)
            xnf4 = [sb("xnf4_%d" % i, [128, D], BF16) for i in range(G)]
            sgs = [sb("sg4_%d" % i, [128, GT_MAX], F32) for i in range(2)]
            st4 = sb("st4", [128, 3 * G], F32)
            pTs4 = [ps("pT4_%d" % i, [128, 8, 128], BF16) for i in range(2)]
            pgs = [ps("pg4_%d" % i, [128, 512], F32) for i in range(2)]
            pus = [ps("pu4_%d" % i, [128, 512], F32) for i in range(2)]
            py = ps("py4", [128, D], F32)
            groups = [list(range(a, min(NOWN, a + G))) for a in range(0, NOWN, G)]
            NG = len(groups)

            def stats4(xap, xk, c0, xn_t, xnk):
                I('act', 'activation', xn_t[:], xap, AF.Square, accum_out=st4[:, c0:c0 + 1], reads=[xk], writes=[xnk, 'st4'])
                rstd_from_ss(st4[:, c0:c0 + 1], st4[:, c0 + 2:c0 + 3], D, ['st4'], ['st4'], st4[:, c0 + 1:c0 + 2])
                I('dve', 'tensor_scalar', xn_t[:], xap, st4[:, c0 + 2:c0 + 3], None, ALU.mult, reads=['st4', xk], writes=[xnk])

            k.capture_start()
            for gi_, ois in enumerate(groups):
                ng = len(ois); GT = ng * 128; o0 = ois[0]
                xt, xk = xts[gi_ % 2], 'xg4_%d' % (gi_ % 2)
                hTg, hTgk = hTs4[gi_ % 2], 'hT4_%d' % (gi_ % 2)
                k.task(('L', gi_))
                dma('sp', xt[:, 0:ng, :], x1_d[o0:o0 + ng].rearrange("n p d -> p n d"), reads=['x1_d%d' % o for o in ois], writes=[xk])
                dma('sp', mix[:, 0:ng, 0:512], oa_d[o0:o0 + ng].rearrange("n p d -> p n d"), reads=['oa_d%d' % o for o in ois], writes=['mix'])
                dma('sp', mix[:, 0:ng, 512:1024], ob_d[o0:o0 + ng].rearrange("n p d -> p n d"), reads=['ob_d%d' % o for o in ois], writes=['mix'])
                k.task(('Ma', gi_))
                for s in range(ng):
                    for kc in range(8):
                        I('pe', 'transpose', pTs4[s][:, kc, :], mix[:, s, kc * 128:(kc + 1) * 128], ident_b[:], reads=['mix', 'ident_b'], writes=['pT4_%d' % s])
                    I('act', 'activation', hTg[:, :, s * 128:(s + 1) * 128], pTs4[s][:, :, :], AF.Identity, reads=['pT4_%d' % s], writes=[hTgk])
                k.task(('Mb', gi_))
                for s in range(ng):
                    for half in range(2):
                        for kc in range(8):
                            I('pe', 'matmul', py[:, half * 512:(half + 1) * 512], hTg[:, kc, s * 128:(s + 1) * 128], Wm[:, kc, half * 512:(half + 1) * 512],
                              start=(kc == 0), stop=(kc == 7), reads=[hTgk, 'Wm'], writes=['py'])
                    I('dve', 'tensor_tensor', xt[:, s, :], xt[:, s, :], py[:], ALU.add, reads=['py', xk], writes=[xk])
                k.task(('Fa', gi_))
                for s in range(ng):
                    stats4(xt[:, s, :], xk, 3 * s, xnf4[s], 'xnf4_%d' % s)
                k.task(('Fb', gi_))
                for s in range(ng):
                    tr_evac(xnf4[s], 'xnf4_%d' % s, 2, pTs4[s], 'pT4_%d' % s, lambda kc, s=s, hTg=hTg: hTg[:, kc, s * 128:(s + 1) * 128], hTgk)
                k.task(('GU1', gi_))
                ffn_group(hTg, hTgk, GT, Wi, 'Wi2', aT, pgs, pus, sgs, gi_, fcs=range(0, NFC // 2))
                k.task(('GU2', gi_))
                ffn_group(hTg, hTgk, GT, Wi, 'Wi2', aT, pgs, pus, sgs, gi_, fcs=range(NFC // 2, NFC))
                k.task(('D', gi_))
                for s in range(ng):
                    ffn_down(aT, s, Wo, 'Wo2', py)
                    I('dve', 'tensor_tensor', xt[:, s, :], xt[:, s, :], py[:], ALU.add, reads=['py', xk], writes=[xk])
                    dma('sp', out[(o0 + s) * 128:(o0 + s + 1) * 128, :], xt[:, s, :], reads=[xk], writes=['out%d' % (o0 + s)])
            T4 = dict(k.capture); k.capture = None; k.cur = None
            order = []
            if NG:
                order += [('L', 0), ('Ma', 0), ('Mb', 0), ('Fa', 0), ('Fb', 0)]
                if NG > 1:
                    order.append(('L', 1))
            for g in range(NG):
                if g + 1 < NG:
                    order.append(('Ma', g + 1))
                order.append(('GU1', g))
                if g + 1 < NG:
                    order.append(('Mb', g + 1))
                order.append(('GU2', g))
                if g + 1 < NG:
                    order.append(('Fa', g + 1))
                order.append(('D', g))
                if g + 1 < NG:
                    order.append(('Fb', g + 1))
                if g + 2 < NG:
                    order.append(('L', g + 2))
            assert sorted(order) == sorted(T4.keys()), (len(order), len(T4))
            for key_ in order:
                k.replay(T4[key_])
            k.es = es0
        k.barrier()

        outkeys = ['out%d' % o for o in range(NOWN)]
        for nm, (src, shape, dt) in dbg.items():
            dd = nc.dram_tensor("dbg_" + nm, list(shape), dt, kind="ExternalOutput").ap()
            if nm == 'modT':
                dma('sp', dd, src[:], reads=['modT'], writes=['dbg_' + nm])
            else:
                allk = [kk_ for kk_ in k.last_w if isinstance(kk_, str) and kk_.startswith(nm)]
                dma('sp', dd, src, reads=allk, writes=['dbg_' + nm])
            outkeys.append('dbg_' + nm)
        k.finish(outkeys)
    return nc, list(dbg.keys())


def _consts():
    f = np.float32
    idx = np.arange(128)
    same = (idx[:, None] // 64) == (idx[None, :] // 64)
    c = {}
    c["c_ident"] = np.eye(128, dtype=f)
    c["c_utri"] = (same & (idx[:, None] <= idx[None, :])).astype(f)
    c["c_maskU"] = np.where(same & (idx[None, :] >= idx[:, None]), 0.0, NEG).astype(f)
    c["c_maskL"] = np.where(same & (idx[None, :] <= idx[:, None]), 0.0, -NEG).astype(f)
    c["c_bones"] = same.astype(f)
    cind = np.zeros((128, 256), f); cind[0:64, 0:128] = 1.0; cind[64:128, 128:256] = 1.0
    c["c_cind"] = cind
    invf = (10000.0 ** (-(np.arange(32, dtype=f) / f(32)))).astype(f)
    c["c_invf"] = np.broadcast_to((invf / f(TWO_PI)).astype(f)[None, :], (128, 32)).copy()
    dm = np.ones((128, 128), f); dm[64:128, 0:64] = 0.0
    c["c_dmask"] = np.tile(dm, (1, 4))
    return c


_PROG_CACHE = {}


def _get_prog(NB, stop_after=99):
    key = (NB, stop_after)
    if key not in _PROG_CACHE:
        _PROG_CACHE[key] = build_program(NB, stop_after=stop_after)
    return _PROG_CACHE[key]


def kernel(x, c, positions, w_ada, b_ada, ffn1_w_in, ffn1_w_out, w_in, gdn_conv_w, gdn_a_log, gdn_dt_bias, gdn_norm_w,
           mla_q_norm_w, mla_w_uq, mla_kv_norm_w, mla_w_ukv, qkn_q_nope, qkn_q_rope, qkn_k_nope, qkn_k_rope,
           mla_out_norm_w, w_out, ffn2_w_in, ffn2_w_out, _stop_after=99):
    f = np.float32
    A = lambda a: np.ascontiguousarray(np.asarray(a))
    x = A(x); B, S, _ = x.shape
    nblk = S // 128
    NB = nblk + 1
    NOWN = (NB + 1) // 2
    bc = lambda v, n=128: np.ascontiguousarray(np.broadcast_to(A(v).reshape(1, -1), (n, A(v).size))).astype(f)
    shared = dict(_consts())
    shared.update({
        "w_ada": A(w_ada)[0], "b_adaT": A(A(b_ada)[0].reshape(72, 128).T),
        "b_gate": A(np.concatenate([A(b_ada)[0][(3 * i + 2) * D:(3 * i + 3) * D] for i in range(3)])[None, :]),
        "ffn1_w_in": A(ffn1_w_in)[0], "ffn1_w_out": A(ffn1_w_out)[0], "ffn2_w_in": A(ffn2_w_in)[0], "ffn2_w_out": A(ffn2_w_out)[0],
        "w_in": A(w_in)[0], "conv_c": A(A(gdn_conv_w)[0].T.reshape(12, 128, 4).transpose(1, 0, 2)),
        "alog_bc": bc(A(gdn_a_log)[0]), "dtb_bc": bc(A(gdn_dt_bias)[0]),
        "gnw_bc": bc(np.tile(A(gdn_norm_w)[0], 4)), "qnw_col": A(A(mla_q_norm_w)[0].reshape(3, 128).T),
        "kvnw_col": A(A(mla_kv_norm_w)[0].reshape(2, 128).T), "w_uq": A(mla_w_uq)[0], "w_ukv": A(mla_w_ukv)[0],
        "gqn_col": A(A(qkn_q_nope)[0].reshape(128, 1)), "gkn_col": A(A(qkn_k_nope)[0].reshape(128, 1)),
        "gqr_bc": bc(A(qkn_q_rope)[0]), "gkr_bc": bc(A(qkn_k_rope)[0]),
        "onw_bc": bc(np.tile(A(mla_out_norm_w)[0], 4)), "w_out": A(w_out)[0],
    })
    in_maps = []
    pos = A(positions).astype(np.int32)
    for b in range(B):
        for p in range(2):
            xs = np.zeros((NB * 128, D), f)
            xs[p * 128:p * 128 + S] = x[b]
            ps_ = np.zeros((NB * 128,), np.int32)
            ps_[p * 128:p * 128 + S] = pos[b]
            m = dict(shared)
            m["xs"] = xs
            m["c_col"] = A(A(c)[b].reshape(8, 128).T)
            m["pos_col"] = A(ps_.reshape(NB, 128).T)
            fl = np.zeros((128, 2), f)
            fl[:, 0] = 1.0 if p == 0 else 0.0
            fl[:, 1] = 0.0 if p == 0 else NEG
            m["flags"] = fl
            in_maps.append(m)
    nc, dbgkeys = _get_prog(NB, _stop_after)
    res = run_bass_kernel_spmd(nc, in_maps, core_ids=list(range(len(in_maps))))
    if _stop_after != 99:
        return res.results
    outp = np.zeros((B, S, D), f)
    for b in range(B):
        for p in range(2):
            o = res.results[b * 2 + p]["out"].reshape(NOWN, 128, D)
            for oi in range(NOWN):
                blk = 2 * oi - p
                if 0 <= blk < nblk:
                    outp[b, blk * 128:(blk + 1) * 128] = o[oi]
    return outp
```

```python
import numpy as np
from contextlib import ExitStack
import concourse.bass as bass
import concourse.mybir as mybir
from concourse.bass_utils import run_bass_kernel_spmd

F32 = mybir.dt.float32
BF16 = mybir.dt.bfloat16
I32 = mybir.dt.int32
AF = mybir.ActivationFunctionType
ALU = mybir.AluOpType

SAME_ENGINE_SYNC = True


class KB:
    NDS = 12

    def __init__(self, nc, es):
        self.nc, self.es = nc, es
        self.engobj = {'pe': nc.tensor, 'act': nc.scalar, 'dve': nc.vector,
                       'pool': nc.gpsimd, 'sp': nc.sync}
        self.prog = {e: [] for e in self.engobj}
        self.sems, self.cnt = {}, {}
        for e in self.engobj:
            self.sems[e] = es.enter_context(nc.semaphore("s_" + e))
            self.cnt[e] = 0
        for q in ('sp', 'pool'):
            for i in range(self.NDS):
                key = ('d', q, i)
                self.sems[key] = es.enter_context(nc.semaphore("d_%s_%d" % (q, i)))
                self.cnt[key] = 0
        self.dnext = {'sp': 0, 'pool': 0}
        self.waited = {e: {} for e in self.engobj}
        self.last_w, self.readers = {}, {}
        self.ninst = 0
        self.flag = {e: set() for e in self.engobj}
        self.rank = None

    def sb(self, name, shape, dt):
        return self.es.enter_context(self.nc.sbuf_tensor(name, list(shape), dt))

    def ps(self, name, shape, dt):
        return self.es.enter_context(self.nc.psum_tensor(name, list(shape), dt))

    def _deps(self, reads, writes):
        deps = []
        for b in reads:
            if b in self.last_w:
                deps.append(self.last_w[b])
        for b in writes:
            if b in self.last_w:
                deps.append(self.last_w[b])
            deps.extend(self.readers.get(b, ()))
        return deps

    def _waits(self, eng, deps):
        need = {}
        for key, val in deps:
            if key == eng and (eng == 'pe' or not SAME_ENGINE_SYNC):
                continue
            if self.waited[eng].get(key, 0) >= val:
                continue
            if need.get(key, 0) < val:
                need[key] = val
        for key, val in need.items():
            self.waited[eng][key] = val
            sem, eo = self.sems[key], self.engobj[eng]
            if key in self.engobj:
                self.flag[key].add(val)
                self.prog[eng].append(lambda eo=eo, sem=sem, key=key, val=val: eo.wait_ge(sem, self.rank[key][val]))
            else:
                self.prog[eng].append(lambda eo=eo, sem=sem, val=val: eo.wait_ge(sem, val))

    def _record(self, stamp, reads, writes):
        for b in writes:
            self.last_w[b] = stamp
            self.readers[b] = []
        for b in reads:
            if b in writes:
                continue
            self.readers.setdefault(b, []).append(stamp)

    muted = False
    capture = None
    cur = None

    def capture_start(self):
        self.capture = []
        self.cur = None

    def task(self, tag):
        if self.capture is not None:
            self.cur = []
            self.capture.append((tag, self.cur))

    def flush(self):
        tasks, self.capture, self.cur = self.capture, None, None
        if not tasks:
            return
        g = [t for tag, t in tasks if tag == 'g']
        m = [t for tag, t in tasks if tag == 'm']
        G = [c for t in g for c in t]
        M = [c for t in m for c in t]
        merged = []
        mi = 0
        for gi, c in enumerate(G):
            merged.append(c)
            while mi < len(M) and (mi + 1) * len(G) <= (gi + 1) * len(M):
                merged.append(M[mi]); mi += 1
        merged.extend(M[mi:])
        self.replay(merged)

    def replay(self, t):
        for c in t:
            if c[0] == 'I':
                self.I(c[1], c[2], *c[3], reads=c[4], writes=c[5], **c[6])
            else:
                self.dma(c[1], c[2], c[3], reads=c[4], writes=c[5])

    keymap = {}

    @staticmethod
    def merge(streams):
        items = []
        for si, st in enumerate(streams):
            n = len(st)
            for i, c in enumerate(st):
                items.append(((i + 0.5) / n, si, i, c))
        items.sort(key=lambda t: (t[0], t[1], t[2]))
        return [t[3] for t in items]

    def I(self, eng, op, *args, reads=(), writes=(), **kw):
        if self.muted:
            return
        if self.keymap:
            reads = tuple(self.keymap.get(x, x) for x in reads); writes = tuple(self.keymap.get(x, x) for x in writes)
        if self.capture is not None:
            self.cur.append(('I', eng, op, args, tuple(reads), tuple(writes), kw))
            return
        deps = self._deps(reads, writes)
        banks = self._psum_banks(list(args) + list(kw.values()))
        for bk in banks:
            for e2, idx2 in self.bank_last.get(bk, {}).items():
                if e2 != eng:
                    deps.append((e2, idx2))
        self._waits(eng, deps)
        self.cnt[eng] += 1
        eo, sem = self.engobj[eng], self.sems[eng]

        def emit(eo=eo, sem=sem, op=op, args=args, kw=kw, eng=eng, idx=self.cnt[eng]):
            inst = getattr(eo, op)(*args, **kw)
            if idx in self.flag[eng]:
                inst.then_inc(sem, 1)
        self.prog[eng].append(emit)
        self._record((eng, self.cnt[eng]), reads, writes)
        for bk in banks:
            self.bank_last.setdefault(bk, {})[eng] = self.cnt[eng]
        self.ninst += 1

    bank_last = None

    def _psum_banks(self, objs):
        if self.bank_last is None:
            self.bank_last = {}
        out = set()
        for a in objs:
            if getattr(a, 'space', None) is None or str(a.space) != 'PSUM':
                continue
            sz = 2 if a.dtype == BF16 else 4
            lo = a.offset
            hi = lo + sum((n - 1) * abs(st) for st, n in list(a.ap)[1:])
            for b in range(lo * sz // 2048, hi * sz // 2048 + 1):
                out.add((a.name, b))
        return out

    def dma(self, q, out, in_, reads=(), writes=()):
        if self.muted:
            return
        if self.keymap:
            reads = tuple(self.keymap.get(x, x) for x in reads); writes = tuple(self.keymap.get(x, x) for x in writes)
        if self.capture is not None:
            self.cur.append(('D', q, out, in_, tuple(reads), tuple(writes)))
            return
        slot = self.dnext[q]
        self.dnext[q] = (slot + 1) % self.NDS
        key = ('d', q, slot)
        deps = self._deps(reads, writes)
        if self.cnt[key]:
            deps.append((key, self.cnt[key]))
        self._waits(q, deps)
        self.cnt[key] += 16
        eo, sem = self.engobj[q], self.sems[key]

        def emit(eo=eo, sem=sem, out=out, in_=in_):
            eo.dma_start(out=out, in_=in_).then_inc(sem, 16)
        self.prog[q].append(emit)
        self._record((key, self.cnt[key]), reads, writes)
        self.ninst += 1

    def barrier(self):
        deps = [(key, val) for key, val in self.cnt.items() if val]
        for e in self.engobj:
            self._waits(e, [(k_, v) for (k_, v) in deps if k_ != e])

    def finish(self, out_keys):
        deps = [self.last_w[b] for b in out_keys if b in self.last_w]
        self._waits('sp', deps)
        self.rank = {e: {idx: i + 1 for i, idx in enumerate(sorted(self.flag[e]))} for e in self.engobj}
        nc = self.nc
        with nc.allow_non_contiguous_dma(reason="small strided layout loads"), \
                nc.allow_low_precision("bf16 matmul operands, fp32 accumulation"):
            with nc.Block() as block:
                @block.tensor
                def _(e):
                    for f in self.prog['pe']:
                        f()

                @block.scalar
                def _(e):
                    for f in self.prog['act']:
                        f()

                @block.vector
                def _(e):
                    for f in self.prog['dve']:
                        f()

                @block.gpsimd
                def _(e):
                    for f in self.prog['pool']:
                        f()

                @block.sync
                def _(e):
                    for f in self.prog['sp']:
                        f()


D = 1024
DFF = 2816
NFC = DFF // 128
EPS = 1e-6
NIN = 2760
OFF_Q, OFF_K, OFF_V, OFF_Z, OFF_A, OFF_B, OFF_CQ, OFF_CKV, OFF_KR = 0, 512, 1024, 1536, 2048, 2052, 2056, 2440, 2696
NEG = -30000.0
TWO_PI = 6.283185307179586


import os
P2CUT = int(os.environ.get('P2CUT', '99'))


def build_program(NB, stop_after=99, G=2, GDT=F32):
    nc = bass.Bass("TRN2", target_bir_lowering=False)
    NT = NB * 128
    OWN = [j for j in range(NB) if j % 2 == 0]
    NOWN = len(OWN)

    def din(name, shape, dt=F32):
        return nc.dram_tensor(name, list(shape), dt, kind="ExternalInput").ap()

    def dscr(name, shape, dt):
        return nc.dram_tensor(name, list(shape), dt).ap()

    xs = din("xs", [NT, D])
    c_col = din("c_col", [128, 8])
    pos_col = din("pos_col", [128, NB], I32)
    flags = din("flags", [128, 2])
    w_ada = din("w_ada", [D, 9 * D])
    b_adaT = din("b_adaT", [128, 72])
    b_gate = din("b_gate", [1, 3 * D])
    w1i = din("ffn1_w_in", [D, 2 * DFF]); w1o = din("ffn1_w_out", [DFF, D])
    w2i = din("ffn2_w_in", [D, 2 * DFF]); w2o = din("ffn2_w_out", [DFF, D])
    w_in = din("w_in", [D, NIN])
    conv_c = din("conv_c", [128, 12, 4])
    alog_bc = din("alog_bc", [128, 4]); dtb_bc = din("dtb_bc", [128, 4])
    gnw_bc = din("gnw_bc", [128, 512])
    qnw_col = din("qnw_col", [128, 3]); kvnw_col = din("kvnw_col", [128, 2])
    w_uq = din("w_uq", [384, 768]); w_ukv = din("w_ukv", [256, 1024])
    gqn_col = din("gqn_col", [128, 1]); gkn_col = din("gkn_col", [128, 1])
    gqr_bc = din("gqr_bc", [128, 64]); gkr_bc = din("gkr_bc", [128, 64])
    onw_bc = din("onw_bc", [128, 512])
    w_o = din("w_out", [D, D])
    c_ident = din("c_ident", [128, 128]); c_utri = din("c_utri", [128, 128])
    c_maskU = din("c_maskU", [128, 128]); c_maskL = din("c_maskL", [128, 128])
    c_bones = din("c_bones", [128, 128]); c_cind = din("c_cind", [128, 256])
    c_invf = din("c_invf", [128, 32]); c_dmask = din("c_dmask", [128, 512])
    out = nc.dram_tensor("out", [NOWN * 128, D], F32, kind="ExternalOutput").ap()

    gB_d = dscr("gB_d", [3, 128, D], F32)
    x1_d = dscr("x1_d", [NOWN, 128, D], F32)
    h2T_d = dscr("h2T_d", [NB, 128, 8 * 128], BF16)
    KT_d = dscr("KT_d", [NB, 128, 512], BF16)
    V_d = dscr("V_d", [NB, 128, 4 * 130], BF16)
    krT_d = dscr("krT_d", [NB, 64, 128], BF16)
    QTn_d = dscr("QTn_d", [NOWN, 128, 512], BF16)
    QTr_d = dscr("QTr_d", [NOWN, 64, 512], BF16)
    oa_d = dscr("oa_d", [NOWN, 128, 512], BF16)
    ob_d = dscr("ob_d", [NOWN, 128, 512], BF16)
    dbg = {}

    with ExitStack() as es:
        k = KB(nc, es)
        I, dma, sb, ps = k.I, k.dma, k.sb, k.ps

        ident_f = sb("ident_f", [128, 128], F32); ident_b = sb("ident_b", [128, 128], BF16)
        ones_f = sb("ones_f", [128, 128], F32)
        modT = sb("modT", [128, 72], F32); s1p = sb("s1p", [128, 72], F32)
        flg = sb("flg", [128, 2], F32)
        dma('sp', ident_f[:], c_ident, writes=['ident_f'])
        dma('pool', ident_b[:], c_ident, writes=['ident_b'])
        dma('sp', flg[:], flags, writes=['flg'])
        I('dve', 'memset', ones_f[:], 1.0, writes=['ones_f'])

        def rstd_from_ss(ss_ap, out_ap, n, keys_r, keys_w, tmp_ap):
            w_ = ss_ap.shape[1]
            I('dve', 'tensor_scalar', tmp_ap, ss_ap, 1.0 / n, EPS, ALU.mult, ALU.add, reads=keys_r, writes=keys_w)
            I('pool', 'tensor_tensor', out_ap, tmp_ap, mhalf[:, 0:w_], ALU.pow, reads=keys_w + ['mhalf'], writes=keys_w)

        eps_col = sb("eps_col", [128, 1], F32)
        mhalf = sb("mhalf", [128, 8], F32)
        I('dve', 'memset', mhalf[:], -0.5, writes=['mhalf'])
        I('dve', 'memset', eps_col[:], EPS, writes=['eps_col'])

        with ExitStack() as p0:
            es0 = k.es; k.es = p0
            ccol = sb("ccol", [128, 8], F32); scb = sb("scb", [128, 8], BF16)
            wA = [sb("wA%d" % i, [128, 8, 1024], BF16) for i in range(2)]
            badT = sb("badT", [128, 72], F32)
            bg = sb("bg", [1, 3 * D], F32); grow = sb("grow", [1, D], F32); gBt = sb("gBt", [128, D], F32)
            pmod = ps("pmod", [128, 512], F32); prow = ps("prow", [1, 512], F32); pB = ps("pB", [128, 512], F32)
            dma('sp', ccol[:], c_col, writes=['ccol'])
            dma('sp', badT[:], b_adaT, writes=['badT'])
            dma('sp', bg[:], b_gate, writes=['bg'])
            I('act', 'activation', scb[:], ccol[:], AF.Silu, reads=['ccol'], writes=['scb'])
            wsrc = w_ada.rearrange("(kc p) n -> p kc n", p=128)
            for cg in range(9):
                wt, wk_ = wA[cg % 2], 'wA%d' % (cg % 2)
                dma('pool', wt[:], wsrc[:, :, cg * 1024:(cg + 1) * 1024], writes=[wk_])
                for fc in range(8):
                    for kc in range(8):
                        I('pe', 'matmul', pmod[:, cg * 8 + fc: cg * 8 + fc + 1], wt[:, kc, fc * 128:(fc + 1) * 128],
                          scb[:, kc:kc + 1], start=(kc == 0), stop=(kc == 7), reads=[wk_, 'scb'], writes=['pmod'])
                if cg % 3 == 2:
                    gi = cg // 3
                    for half in range(2):
                        for kc in range(8):
                            I('pe', 'matmul', prow[:, :], scb[:, kc:kc + 1], wt[:, kc, half * 512:(half + 1) * 512],
                              start=(kc == 0), stop=(kc == 7), reads=[wk_, 'scb'], writes=['prow'])
                        I('dve', 'tensor_tensor', grow[:, half * 512:(half + 1) * 512], prow[:, :],
                          bg[:, gi * D + half * 512: gi * D + (half + 1) * 512], ALU.add, reads=['prow', 'bg'], writes=['grow'])
                    for half in range(2):
                        I('pe', 'matmul', pB[:, :], ones_f[0:1, :], grow[0:1, half * 512:(half + 1) * 512], start=True, stop=True,
                          reads=['ones_f', 'grow'], writes=['pB'])
                        I('act', 'activation', gBt[:, half * 512:(half + 1) * 512], pB[:, :], AF.Identity,
                          scale=(1.0 if gi == 1 else 0.5), reads=['pB'], writes=['gBt'])
                    dma('sp', gB_d[gi], gBt[:], reads=['gBt'], writes=['gB_d%d' % gi])
            I('dve', 'tensor_tensor', modT[:], pmod[:, 0:72], badT[:], ALU.add, reads=['pmod', 'badT'], writes=['modT'])
            I('dve', 'tensor_scalar', s1p[:], modT[:], 1.0, None, ALU.add, reads=['modT'], writes=['s1p'])
            k.es = es0
        k.barrier()
        if stop_after == 0:
            dbg['modT'] = (modT, [128, 72], F32)

        def SH(n):
            return modT[:, (3 * n) * 8:(3 * n) * 8 + 8]

        def SC(n):
            return s1p[:, (3 * n + 1) * 8:(3 * n + 1) * 8 + 8]

        def load_w_bf16(tile, key, src, rows_chunks, cols, c0=0, piece=1024):
            s = src.rearrange("(kc p) n -> p kc n", p=128)
            for a in range(0, cols, piece):
                b = min(cols, a + piece)
                dma('pool', tile[:, :, a:b], s[:, :, c0 + a:c0 + b], writes=[key])

        def norm_T(xap, nrm, hT_ap_fn, ss_ap, tmp_ap, rstd_ap, xn, xnk, junk, pT, pTk, hTk, statk):
            I('act', 'activation', junk[:], xap, AF.Square, accum_out=ss_ap, reads=[statk + '_x'], writes=['junk', statk])
            rstd_from_ss(ss_ap, rstd_ap, D, [statk], [statk], tmp_ap)
            I('dve', 'tensor_scalar', xn[:], xap, rstd_ap, None, ALU.mult, reads=[statk, statk + '_x'], writes=[xnk])
            for kc in range(8):
                I('pe', 'transpose', pT[:, kc, :], xn[:, kc * 128:(kc + 1) * 128], ident_b[:], reads=[xnk, 'ident_b'], writes=[pTk])
            for kc in range(8):
                I('act', 'activation', hT_ap_fn(kc), pT[:, kc, :], AF.Identity, scale=SC(nrm)[:, kc:kc + 1],
                  bias=SH(nrm)[:, kc:kc + 1], reads=[pTk, 's1p', 'modT'], writes=[hTk])

        def ffn_group(hT, hTk, GT, Wi, Wik, aT, pgs, pus, sgs, it, fcs=None):
            for fc in (range(NFC) if fcs is None else fcs):
                b = (it * NFC + fc) % 2
                pg, pu, sg = pgs[b], pus[b], sgs[b]
                for kc in range(8):
                    I('pe', 'matmul', pg[:, 0:GT], Wi[:, kc, fc * 128:(fc + 1) * 128], hT[:, kc, 0:GT], start=(kc == 0), stop=(kc == 7),
                      reads=[Wik, hTk], writes=['pg%d' % b])
                for kc in range(8):
                    I('pe', 'matmul', pu[:, 0:GT], Wi[:, kc, DFF + fc * 128:DFF + (fc + 1) * 128], hT[:, kc, 0:GT], start=(kc == 0), stop=(kc == 7),
                      reads=[Wik, hTk], writes=['pu%d' % b])
                I('act', 'activation', sg[:, 0:GT], pg[:, 0:GT], AF.Silu, reads=['pg%d' % b], writes=['sg%d' % b])
                I('dve', 'tensor_tensor', aT[:, fc, 0:GT], sg[:, 0:GT], pu[:, 0:GT], ALU.mult, reads=['sg%d' % b, 'pu%d' % b], writes=['aT'])

        def ffn_down(aT, s, Wo, Wok, py):
            for half in range(2):
                for fc in range(NFC):
                    I('pe', 'matmul', py[:, half * 512:(half + 1) * 512], aT[:, fc, s * 128:(s + 1) * 128], Wo[:, fc, half * 512:(half + 1) * 512],
                      start=(fc == 0), stop=(fc == NFC - 1), reads=['aT', Wok], writes=['py'])

        GT_MAX = G * 128
        with ExitStack() as p1:
            es0 = k.es; k.es = p1
            Wi = sb("Wi", [128, 8, 2 * DFF], BF16); Wo = sb("Wo", [128, NFC, D], BF16)
            gB = sb("gB", [128, D], F32)
            xg = [sb("xg%d" % i, [128, G, D], F32) for i in range(2)]
            hT = sb("hT", [128, 8, GT_MAX], BF16)
            aT = sb("aT", [128, NFC, GT_MAX], BF16)
            h2o = [sb("h2o%d" % i, [128, 8, 128], BF16) for i in range(2)]
            xn = sb("xn", [128, D], BF16); junk = sb("junk", [128, D], BF16)
            yg = sb("yg", [128, D], F32)
            sgs = [sb("sg%d" % i, [128, GT_MAX], F32) for i in range(2)]
            st = sb("st", [128, 8], F32)
            pT = ps("pT", [128, 8, 128], BF16)
            pgs = [ps("pg%d" % i, [128, 512], F32) for i in range(2)]
            pus = [ps("pu%d" % i, [128, 512], F32) for i in range(2)]
            py = ps("py", [128, D], F32)
            load_w_bf16(Wi, 'Wi', w1i, 8, 2 * DFF)
            load_w_bf16(Wo, 'Wo', w1o, NFC, D)
            dma('sp', gB[:], gB_d[0], reads=['gB_d0'], writes=['gB'])
            groups = [list(range(a, min(NB, a + G))) for a in range(0, NB, G)] if stop_after >= 1 else []
            NG = len(groups)
            hTs = [hT, sb("hT_b", [128, 8, GT_MAX], BF16)]
            xnf = [sb("xnf%d" % i, [128, D], BF16) for i in range(G)]
            xn2 = [sb("xn2_%d" % i, [128, D], BF16) for i in range(G)]
            pTs = [pT, ps("pT_b", [128, 8, 128], BF16)]
            st12 = sb("st12", [128, 6 * G + 6], F32)

            def stats(xap, xk, c0, xn_t, xnk):
                I('act', 'activation', junk[:], xap, AF.Square, accum_out=st12[:, c0:c0 + 1], reads=[xk], writes=['junk', 'st12'])
                rstd_from_ss(st12[:, c0:c0 + 1], st12[:, c0 + 2:c0 + 3], D, ['st12'], ['st12'], st12[:, c0 + 1:c0 + 2])
                I('dve', 'tensor_scalar', xn_t[:], xap, st12[:, c0 + 2:c0 + 3], None, ALU.mult, reads=['st12', xk], writes=[xnk])

            def tr_evac(xn_t, xnk, nrm, pT_t, pTk, dst_fn, dstk):
                for kc in range(8):
                    I('pe', 'transpose', pT_t[:, kc, :], xn_t[:, kc * 128:(kc + 1) * 128], ident_b[:], reads=[xnk, 'ident_b'], writes=[pTk])
                for kc in range(8):
                    I('dve', 'tensor_scalar', dst_fn(kc), pT_t[:, kc, :], SC(nrm)[:, kc:kc + 1], SH(nrm)[:, kc:kc + 1], ALU.mult, ALU.add,
                      reads=[pTk, 's1p', 'modT'], writes=[dstk])

            k.capture_start()
            for gi_, blks in enumerate(groups):
                xt, xk = xg[gi_ % 2], 'xg%d' % (gi_ % 2)
                hTg, hTgk = hTs[gi_ % 2], 'hT%d' % (gi_ % 2)
                ng = len(blks); GT = ng * 128
                k.task(('L', gi_))
                dma('sp', xt[:, 0:ng, :], xs[blks[0] * 128:(blks[0] + ng) * 128, :].rearrange("(s p) d -> p s d", p=128), writes=[xk])
                k.task(('Fa', gi_))
                for s in range(ng):
                    stats(xt[:, s, :], xk, 3 * s, xnf[s], 'xnf%d' % s)
                k.task(('Fb', gi_))
                for s in range(ng):
                    tr_evac(xnf[s], 'xnf%d' % s, 0, pTs[s], 'pTs%d' % s, lambda kc, s=s, hTg=hTg: hTg[:, kc, s * 128:(s + 1) * 128], hTgk)
                k.task(('GU1', gi_))
                ffn_group(hTg, hTgk, GT, Wi, 'Wi', aT, pgs, pus, sgs, gi_, fcs=range(0, NFC // 2))
                k.task(('GU2', gi_))
                ffn_group(hTg, hTgk, GT, Wi, 'Wi', aT, pgs, pus, sgs, gi_, fcs=range(NFC // 2, NFC))
                k.task(('D', gi_))
                for s in range(ng):
                    blk = blks[s]
                    ffn_down(aT, s, Wo, 'Wo', py)
                    I('dve', 'tensor_tensor', yg[:], py[:], gB[:], ALU.mult, reads=['py', 'gB'], writes=['yg'])
                    I('dve', 'tensor_tensor', xt[:, s, :], xt[:, s, :], yg[:], ALU.add, reads=['yg', xk], writes=[xk])
                    if blk % 2 == 0:
                        dma('sp', x1_d[blk // 2], xt[:, s, :], reads=[xk], writes=['x1_d%d' % (blk // 2)])
                k.task(('Na', gi_))
                for s in range(ng):
                    stats(xt[:, s, :], xk, 3 * G + 3 * s, xn2[s], 'xn2_%d' % s)
                k.task(('Nb', gi_))
                for s in range(ng):
                    blk = blks[s]
                    ho, hk = h2o[s % 2], 'h2o%d' % (s % 2)
                    tr_evac(xn2[s], 'xn2_%d' % s, 1, pTs[s], 'pTs%d' % s, lambda kc, ho=ho: ho[:, kc, :], hk)
                    dma('sp', h2T_d[blk], ho[:].rearrange("p a b -> p (a b)"), reads=[hk], writes=['h2T_d%d' % blk])
            T = dict(k.capture); k.capture = None; k.cur = None
            order = []
            if NG:
                order += [('L', 0), ('Fa', 0), ('Fb', 0)]
                if NG > 1:
                    order += [('L', 1)]
            for g in range(NG):
                order.append(('GU1', g))
                if g + 1 < NG:
                    order.append(('Fa', g + 1))
                order.append(('GU2', g))
                if g >= 1:
                    order.append(('Nb', g - 1))
                if g + 1 < NG:
                    order.append(('Fb', g + 1))
                order += [('D', g), ('Na', g)]
                if g + 2 < NG:
                    order.append(('L', g + 2))
            if NG:
                order.append(('Nb', NG - 1))
            assert sorted(order) == sorted(T.keys()), (len(order), len(T))
            for key_ in order:
                k.replay(T[key_])
            k.es = es0
        k.barrier()
        if stop_after == 1:
            dbg['x1_d'] = (x1_d, [NOWN, 128, D], F32)
            dbg['h2T_d'] = (h2T_d, [NB, 128, 1024], BF16)

        SCALE_Q = 192.0 ** -0.5
        with ExitStack() as p2:
            es0 = k.es; k.es = p2
            Win = sb("Win", [128, 8, NIN], BF16)
            Wuq = sb("Wuq", [128, 3, 768], BF16); Wukv = sb("Wukv", [128, 2, 1024], BF16)
            load_w_bf16(Win, 'Win', w_in, 8, NIN, piece=920)
            load_w_bf16(Wuq, 'Wuq', w_uq, 3, 768)
            load_w_bf16(Wukv, 'Wukv', w_ukv, 2, 1024)

            def cload(name, src, shape, dt=F32):
                t = sb(name, shape, dt)
                dma('sp', t[:], src, writes=[name])
                return t
            utri = cload("utri", c_utri, [128, 128]); maskU = cload("maskU", c_maskU, [128, 128])
            maskLn = cload("maskLn", c_maskL, [128, 128]); bones = cload("bones", c_bones, [128, 128])
            cind = cload("cind", c_cind, [128, 256]); invf = cload("invf", c_invf, [128, 32])
            convc = cload("convc", conv_c, [128, 12, 4]); alog = cload("alog", alog_bc, [128, 4]); dtb = cload("dtb", dtb_bc, [128, 4])
            gnw = cload("gnw", gnw_bc, [128, 512]); qnw = cload("qnw", qnw_col, [128, 3]); kvnw = cload("kvnw", kvnw_col, [128, 2])
            gqn = cload("gqn", gqn_col, [128, 1]); gkn = cload("gkn", gkn_col, [128, 1])
            gqr = cload("gqr", gqr_bc, [128, 64]); gkr = cload("gkr", gkr_bc, [128, 64])
            posi = cload("posi", pos_col, [128, NB], I32)
            posf = sb("posf", [128, NB], F32)
            negA = sb("negA", [128, 4], F32)
            I('act', 'activation', negA[:], alog[:], AF.Exp, reads=['alog'], writes=['negA'])
            I('dve', 'tensor_scalar', negA[:], negA[:], -1.0, None, ALU.mult, reads=['negA'], writes=['negA'])
            I('dve', 'tensor_scalar', gqn[:], gqn[:], SCALE_Q, None, ALU.mult, reads=['gqn'], writes=['gqn'])
            I('dve', 'tensor_copy', posf[:], posi[:], reads=['posi'], writes=['posf'])
            cosT = sb("cosT", [128, NB, 32], F32); sinT = sb("sinT", [128, NB, 32], F32)
            with ExitStack() as pr:
                k.es = pr
                ang = sb("ang", [128, NB, 32], F32); angi = sb("angi", [128, NB, 32], I32)
                fr = sb("fr", [128, NB, 32], F32); gt = sb("gt", [128, NB, 32], F32)
                for b in range(NB):
                    I('dve', 'tensor_scalar', ang[:, b, :], invf[:], posf[:, b:b + 1], None, ALU.mult, reads=['invf', 'posf'], writes=['ang'])
                I('dve', 'tensor_copy', angi[:], ang[:], reads=['ang'], writes=['angi'])
                I('dve', 'tensor_copy', fr[:], angi[:], reads=['angi'], writes=['fr'])
                I('dve', 'tensor_tensor', fr[:], ang[:], fr[:], ALU.subtract, reads=['ang', 'fr'], writes=['fr'])
                for (dst, dk_, add) in ((sinT, 'sinT', 0.0), (cosT, 'cosT', 0.25)):
                    I('dve', 'tensor_scalar', ang[:], fr[:], add, None, ALU.add, reads=['fr'], writes=['ang'])
                    I('dve', 'tensor_scalar', gt[:], ang[:], 0.5, None, ALU.is_gt, reads=['ang'], writes=['gt'])
                    I('dve', 'tensor_tensor', ang[:], ang[:], gt[:], ALU.subtract, reads=['gt', 'ang'], writes=['ang'])
                    I('dve', 'tensor_scalar', gt[:], ang[:], -0.5, None, ALU.is_lt, reads=['ang'], writes=['gt'])
                    I('dve', 'tensor_tensor', ang[:], ang[:], gt[:], ALU.add, reads=['gt', 'ang'], writes=['ang'])
                    I('act', 'activation', dst[:], ang[:], AF.Sin, scale=TWO_PI, reads=['ang'], writes=[dk_])
                k.es = p2
            k.barrier()
            h2 = [sb("h2_%d" % i, [128, 8, 128], BF16) for i in range(2)]
            xbuf = sb("xbuf", [128, 12, 132], BF16); csP = [sb("cs_%d" % i, [128, 8, 128], F32) for i in range(2)]
            diagw = sb("diagw", [128, 12, 4, 128], BF16)
            for ch_ in range(12):
                for tp_ in range(4):
                    I('dve', 'tensor_scalar', diagw[:, ch_, tp_, :], ident_f[:], convc[:, ch_, tp_:tp_ + 1], None, ALU.mult,
                      reads=['ident_f', 'convc'], writes=['diagw'])
            I('dve', 'memset', xbuf[:], 0.0, writes=['xbuf'])
            HB = {}
            for h in range(4):
                for nm in ('gbh', 'gbl', 'vb', 'kbg', 'kdec'):
                    HB[nm, h] = sb("%s_%d" % (nm, h), [128, 128], BF16)
            B4 = {}
            for nm, dt_ in (('sq4q', BF16), ('sq4k', BF16), ('qnT', BF16), ('knT', BF16), ('qdT0', BF16), ('qdT1', BF16),
                            ('rn4q', F32), ('rn4k', F32), ('tU', F32), ('tL', F32), ('expGb', F32), ('E', F32), ('E2', F32)):
                B4[nm] = sb("b4_" + nm, [128, 4, 128], dt_)
            for nm, dt_ in (('X0', F32), ('X1', F32), ('XT0', F32), ('XT1', F32), ('TT', F32), ('TTb', BF16), ('u', F32), ('QKT', BF16),
                            ('wkT0', BF16), ('wkT1', BF16), ('vnew', BF16)):
                B4[nm] = sb("b4_" + nm, [128, 4, 128], dt_)
            for nm in ('qnT', 'knT', 'qdT0', 'qdT1', 'E', 'E2', 'X0', 'X1', 'XT0', 'XT1', 'TT', 'TTb', 'u', 'QKT', 'wkT0', 'wkT1', 'vnew'):
                for h in range(4):
                    HB[nm, h] = B4[nm][:, h, :]
            I('dve', 'memset', B4['qdT0'][:], 0.0, writes=['qdT0_%d' % h for h in range(4)])
            I('dve', 'memset', B4['qdT1'][:], 0.0, writes=['qdT1_%d' % h for h in range(4)])
            I('dve', 'memset', B4['wkT0'][:], 0.0, writes=['wkT0_%d' % h for h in range(4)])
            I('dve', 'memset', B4['wkT1'][:], 0.0, writes=['wkT1_%d' % h for h in range(4)])
            maskU4 = sb("maskU4", [128, 4, 128], F32); maskLn4 = sb("maskLn4", [128, 4, 128], F32); ident4 = sb("ident4", [128, 4, 128], F32)
            for h in range(4):
                dma('sp', maskU4[:, h, :], c_maskU, writes=['maskU4']); dma('sp', maskLn4[:, h, :], c_maskL, writes=['maskLn4'])
                dma('sp', ident4[:, h, :], c_ident, writes=['ident4'])
            S = sb("S", [128, 4, 128], F32); Sb = sb("Sb", [128, 4, 128], BF16)
            I('dve', 'memset', S[:], 0.0, writes=['S_%d' % h for h in range(4)])
            I('dve', 'memset', Sb[:], 0.0, writes=['Sb_%d' % h for h in range(4)])
            ones_b = sb("ones_b", [128, 128], BF16); csvP = [sb("csv_%d" % i, [128, 4, 128], BF16) for i in range(2)]; ghfP = [sb("ghf_%d" % i, [128, 8], F32) for i in range(2)]
            I('dve', 'memset', ones_b[:], 1.0, writes=['ones_b'])
            gstP = [sb("gst_%d" % i, [128, 48], F32) for i in range(2)]
            mlainP = [sb("mlain_%d" % i, [128, 704], F32) for i in range(2)]
            ghl = sb("ghl", [128, 2, 32], BF16)
            I('dve', 'memset', ghl[:], 0.0, writes=['ghl'])
            utri_b = sb("utri_b", [128, 128], BF16); bones_b = sb("bones_b", [128, 128], BF16); cind_b = sb("cind_b", [128, 256], BF16)
            dma('pool', utri_b[:], c_utri, writes=['utri_b']); dma('pool', bones_b[:], c_bones, writes=['bones_b']); dma('pool', cind_b[:], c_cind, writes=['cind_b'])
            zwP = [sb("zw_%d" % i, [128, 512], F32) for i in range(2)]; oa = sb("oa", [128, 512], BF16); osb = sb("osb", [128, 512], F32)
            mtmp = sb("mtmp", [128, 512], F32); mb16 = sb("mb16", [128, 512], BF16)
            cqnT = sb("cqnT", [128, 3, 128], BF16); ckvnT = sb("ckvnT", [128, 2, 128], BF16)
            KTt = sb("KTt", [128, 4, 128], BF16); Vt = sb("Vt", [128, 4, 130], BF16); krTt = sb("krTt", [64, 128], BF16)
            QTnt = sb("QTnt", [128, 4, 128], BF16); QTrt = sb("QTrt", [64, 4, 128], BF16)
            rop = sb("rop", [128, 6, 32], F32); rin = sb("rin", [128, 64], F32); rout = sb("rout", [128, 4, 64], BF16)
            mst = sb("mst", [128, 32], F32); junk2 = sb("junk2", [128, 512], F32)
            I('dve', 'memset', Vt[:], 1.0, writes=['Vt'])
            pp = [ps("pp%d" % i, [128, 512], F32) for i in range(7)]
            ptb = ps("ptb", [128, 8, 128], BF16)
            pgd = pp[0]

            def gslot(h, i):
                return pp[4 + i][:, h * 128:(h + 1) * 128], 'gs%d' % i
            rot = [0] * 4

            def nslot(h):
                rot[h] ^= 1
                return gslot(h, rot[h])

            def hb(nm, h):
                return HB[nm, h], '%s_%d' % (nm, h)

            blocktasks = []
            for blk in range(NB if stop_after >= 2 else 0):
                k.muted = False
                par = blk % 2
                cs, csv, gst, ghf, zw, mlain = csP[par], csvP[par], gstP[par], ghfP[par], zwP[par], mlainP[par]
                km = {'gst': 'gst_%d' % par, 'ghf': 'ghf_%d' % par, 'zw': 'zw_%d' % par, 'mlain': 'mlain_%d' % par}
                km.update({'cs%d' % i: 'cs%d_%d' % (i, par) for i in range(12)})
                km.update({'csv%d' % i: 'csv%d_%d' % (i, par) for i in range(4)})
                k.keymap = km
                k.capture_start(); k.task('p')
                own = (blk % 2 == 0)
                oi = blk // 2
                ht, hk = h2[blk % 2], 'h2_%d' % (blk % 2)
                dma('sp', ht[:].rearrange("p a b -> p (a b)"), h2T_d[blk], reads=['h2T_d%d' % blk], writes=[hk])
                for kc in range(8):
                    I('pe', 'matmul', pp[2][:, 0:392], ht[:, kc, :], Win[:, kc, OFF_A:OFF_A + 392], start=(kc == 0), stop=(kc == 7), reads=[hk, 'Win'], writes=['pp2'])
                for kc in range(8):
                    I('pe', 'matmul', pp[3][:, 0:320], ht[:, kc, :], Win[:, kc, OFF_CKV:OFF_CKV + 320], start=(kc == 0), stop=(kc == 7), reads=[hk, 'Win'], writes=['pp3'])
                I('act', 'activation', mlain[:, 0:384], pp[2][:, 8:392], AF.Identity, reads=['pp2'], writes=['mlain'])
                I('act', 'activation', mlain[:, 384:704], pp[3][:, 0:320], AF.Identity, reads=['pp3'], writes=['mlain'])
                I('dve', 'tensor_tensor', gst[:, 8:12], pp[2][:, 0:4], dtb[:], ALU.add, reads=['pp2', 'dtb'], writes=['gst'])
                I('act', 'activation', gst[:, 8:12], gst[:, 8:12], AF.Exp, reads=['gst'], writes=['gst'])
                I('act', 'activation', gst[:, 8:12], gst[:, 8:12], AF.Ln, bias=1.0, reads=['gst'], writes=['gst'])
                I('dve', 'tensor_tensor', gst[:, 0:4], gst[:, 8:12], negA[:], ALU.mult, reads=['gst', 'negA'], writes=['gst'])
                I('act', 'activation', gst[:, 12:16], pp[2][:, 4:8], AF.Exp, scale=-1.0, reads=['pp2'], writes=['gst'])
                I('dve', 'tensor_scalar', gst[:, 12:16], gst[:, 12:16], 1.0, None, ALU.add, reads=['gst'], writes=['gst'])
                I('dve', 'reciprocal', gst[:, 4:8], gst[:, 12:16], reads=['gst'], writes=['gst'])
                if blk == 0:
                    I('dve', 'tensor_scalar', gst[:, 4:8], gst[:, 4:8], flg[:, 0:1], None, ALU.mult, reads=['gst', 'flg'], writes=['gst'])
                I('dve', 'tensor_copy', ghl[:, 0, 0:4], gst[:, 0:4], reads=['gst'], writes=['ghl'])
                I('dve', 'tensor_tensor', ghl[:, 1, 0:4], gst[:, 0:4], ghl[:, 0, 0:4], ALU.subtract, reads=['gst', 'ghl'], writes=['ghl'])
                I('dve', 'tensor_copy', ghf[:, 0:4], ghl[:, 0, 0:4], reads=['ghl'], writes=['ghf'])
                I('dve', 'tensor_tensor', ghf[:, 4:8], gst[:, 0:4], ghf[:, 0:4], ALU.subtract, reads=['gst', 'ghf'], writes=['ghf'])
                for (cols, lt, ltk) in (((320, 352), utri_b[:], 'utri_b'), ((352, 384), bones_b[:], 'bones_b'),
                                        ((384, 416), cind_b[:, 0:128], 'cind_b'), ((416, 448), cind_b[:, 128:256], 'cind_b')):
                    for part in range(2):
                        I('pe', 'matmul', pgd[:, cols[0]:cols[1]], lt, ghl[:, part, :], start=(part == 0), stop=(part == 1), reads=[ltk, 'ghl'], writes=['pp0_2'])
                I('dve', 'tensor_copy', gst[:, 16:20], pgd[:, 320:324], reads=['pp0_2'], writes=['gst'])
                I('dve', 'tensor_scalar', gst[:, 20:24], pgd[:, 320:324], -1.0, None, ALU.mult, reads=['pp0_2'], writes=['gst'])
                I('act', 'activation', gst[:, 24:28], gst[:, 16:20], AF.Exp, reads=['gst'], writes=['gst'])
                I('dve', 'tensor_tensor', gst[:, 24:28], gst[:, 24:28], gst[:, 4:8], ALU.mult, reads=['gst'], writes=['gst'])
                I('dve', 'tensor_tensor', gst[:, 28:32], pgd[:, 352:356], gst[:, 16:20], ALU.subtract, reads=['pp0_2', 'gst'], writes=['gst'])
                I('act', 'activation', gst[:, 28:32], gst[:, 28:32], AF.Exp, reads=['gst'], writes=['gst'])
                for c in range(2):
                    I('act', 'activation', gst[:, 32 + 4 * c:36 + 4 * c], pgd[:, 384 + 32 * c:388 + 32 * c], AF.Exp, reads=['pp0_2'], writes=['gst'])
                def conv_proj(ch):
                    pq = (pp[0] if ch % 2 == 0 else pp[2])[:, 0:128]; pqk = ('pp0_2' if ch % 2 == 0 else 'pp2')
                    for kc in range(8):
                        I('pe', 'matmul', pq, Win[:, kc, ch * 128:(ch + 1) * 128], ht[:, kc, :], start=(kc == 0), stop=(kc == 7), reads=[hk, 'Win'], writes=[pqk])
                    xk = 'xbuf%d' % ch
                    if blk == 0:
                        I('act', 'activation', xbuf[:, ch, 3:131], pq, AF.Identity, scale=flg[:, 0:1], reads=[pqk, 'flg', 'xbuf'], writes=[xk])
                    else:
                        I('act', 'activation', xbuf[:, ch, 3:131], pq, AF.Identity, reads=[pqk, 'xbuf'], writes=[xk])

                def conv_taps(ch):
                    xk = 'xbuf%d' % ch
                    pc = pp[3][:, 0:128]
                    for tp in range(4):
                        I('pe', 'matmul', pc, diagw[:, ch, tp, :], xbuf[:, ch, tp:tp + 128], start=(tp == 0), stop=(tp == 3),
                          reads=['diagw', xk], writes=['pp3'])
                    I('pool', 'tensor_copy', xbuf[:, ch, 0:3], xbuf[:, ch, 128:131], reads=[xk], writes=[xk])
                    if ch < 8:
                        I('act', 'activation', cs[:, ch, :], pc, AF.Silu, reads=['pp3'], writes=['cs%d' % ch])
                    else:
                        I('act', 'activation', csv[:, ch - 8, :], pc, AF.Silu, reads=['pp3'], writes=['csv%d' % (ch - 8)])

                conv_proj(0)
                for ch in range(12):
                    if ch + 1 < 12:
                        conv_proj(ch + 1)
                    conv_taps(ch)
                if own:
                    for kc in range(8):
                        I('pe', 'matmul', pp[3][:, :], ht[:, kc, :], Win[:, kc, OFF_Z:OFF_Z + 512], start=(kc == 0), stop=(kc == 7), reads=[hk, 'Win'], writes=['pp3'])
                    I('act', 'activation', zw[:], pp[3][:, :], AF.Silu, reads=['pp3'], writes=['zw'])
                    I('dve', 'tensor_tensor', zw[:], zw[:], gnw[:], ALU.mult, reads=['zw', 'gnw'], writes=['zw'])
                k.task('g')
                H4 = range(4)
                for (c0, sqn, bank, bankk, rnn, outn, scl) in ((0, 'sq4q', pp[4], 'gs0', 'rn4q', 'qnT', 128.0 ** -0.5),
                                                                (4, 'sq4k', pp[5], 'gs1', 'rn4k', 'knT', 1.0)):
                    sq, rn, o4 = B4[sqn], B4[rnn], B4[outn]
                    csk = ['cs%d' % (c0 + i) for i in range(4)]
                    csf = cs[:, c0:c0 + 4, :].rearrange("p a b -> p (a b)")
                    I('act', 'activation', sq[:].rearrange("p a b -> p (a b)"), csf, AF.Square, reads=csk, writes=[sqn])
                    I('pe', 'matmul', bank[:, :], ones_b[:], sq[:].rearrange("p a b -> p (a b)"), start=True, stop=True, reads=['ones_b', sqn], writes=[bankk])
                    I('act', 'activation', rn[:].rearrange("p a b -> p (a b)"), bank[:, :], AF.Ln, bias=eps_col[:, 0:1], reads=[bankk, 'eps_col'], writes=[rnn])
                    I('act', 'activation', rn[:].rearrange("p a b -> p (a b)"), rn[:].rearrange("p a b -> p (a b)"), AF.Exp, scale=-0.5, reads=[rnn], writes=[rnn])
                    I('dve', 'scalar_tensor_tensor', o4[:].rearrange("p a b -> p (a b)"), csf, scl, rn[:].rearrange("p a b -> p (a b)"), ALU.mult, ALU.mult,
                      reads=csk + [rnn], writes=['%s_%d' % (outn, h) for h in range(4)])
                k.task('g')
                slots = [nslot(h) for h in H4]
                gbank = pp[4 + rot[0]]; gbankk = 'gs%d' % rot[0]
                for h in H4:
                    gbh, gbhk = hb('gbh', h); gbl, gblk = hb('gbl', h)
                    I('dve', 'tensor_scalar', gbh[:], ones_f[:], ghf[:, h:h + 1], None, ALU.mult, reads=['ones_f', 'ghf'], writes=[gbhk])
                    I('dve', 'tensor_scalar', gbl[:], ones_f[:], ghf[:, 4 + h:5 + h], None, ALU.mult, reads=['ones_f', 'ghf'], writes=[gblk])
                    sl, slk = slots[h]
                    I('pe', 'matmul', sl, gbh[:], utri_b[:], start=True, stop=False, reads=[gbhk, 'utri_b'], writes=[slk])
                    I('pe', 'matmul', sl, gbl[:], utri_b[:], start=False, stop=True, reads=[gblk, 'utri_b'], writes=[slk])
                fl = lambda t: t[:].rearrange("p a b -> p (a b)")
                I('dve', 'tensor_tensor', fl(B4['tU']), gbank[:, :], fl(maskU4), ALU.add, reads=[gbankk, 'maskU4'], writes=['tU4'])
                I('dve', 'tensor_tensor', fl(B4['tL']), gbank[:, :], fl(maskLn4), ALU.add, reads=[gbankk, 'maskLn4'], writes=['tL4'])
                I('act', 'activation', fl(B4['expGb']), gbank[:, :], AF.Exp, reads=[gbankk], writes=['eg4'])
                for h in H4:
                    E, Ek = hb('E', h); E2, E2k = hb('E2', h)
                    I('act', 'activation', E[:], B4['tU'][:, h, :], AF.Exp, bias=gst[:, 20 + h:21 + h], reads=['tU4', 'gst'], writes=[Ek])
                    I('act', 'activation', E2[:], B4['tL'][:, h, :], AF.Exp, scale=-1.0, bias=gst[:, 16 + h:17 + h], reads=['tL4', 'gst'], writes=[E2k])
                E2ks = ['E2_%d' % h for h in H4]
                I('dve', 'tensor_tensor', fl(B4['E2']), fl(B4['E2']), fl(ident4), ALU.subtract, reads=E2ks + ['ident4'], writes=E2ks)
                for c in range(2):
                    I('dve', 'tensor_tensor', B4['qdT%d' % c][:, :, c * 64:(c + 1) * 64], B4['qnT'][:, :, c * 64:(c + 1) * 64],
                      B4['expGb'][:, :, c * 64:(c + 1) * 64], ALU.mult, reads=['qnT_%d' % h for h in H4] + ['eg4'],
                      writes=['qdT%d_%d' % (c, h) for h in H4])
                k.task('g')
                K4 = lambda nm: ['%s_%d' % (nm, h) for h in H4]
                slA = [nslot(h) for h in H4]
                for h in H4:
                    kn, knk = hb('knT', h)
                    I('pe', 'matmul', slA[h][0], kn[:], kn[:], start=True, stop=True, reads=[knk], writes=[slA[h][1]])
                for h in H4:
                    E2, E2k = hb('E2', h); X0, X0k = hb('X0', h)
                    I('dve', 'scalar_tensor_tensor', X0[:], slA[h][0], gst[:, 4 + h:5 + h], E2[:], ALU.mult, ALU.mult, reads=[slA[h][1], 'gst', E2k], writes=[X0k])
                slB = [nslot(h) for h in H4]
                bkB = pp[4 + rot[0]]; bkBk = 'gs%d' % rot[0]
                for h in H4:
                    kn, knk = hb('knT', h); qn, qnk = hb('qnT', h)
                    I('pe', 'matmul', slB[h][0], kn[:], qn[:], start=True, stop=True, reads=[knk, qnk], writes=[slB[h][1]])
                I('dve', 'tensor_tensor', fl(B4['QKT']), bkB[:, :], fl(B4['E']), ALU.mult, reads=[bkBk] + K4('E'), writes=K4('QKT'))
                k.task('g')
                for h in H4:
                    X0, X0k = hb('X0', h)
                    sl3, sl3k = nslot(h)
                    I('pe', 'transpose', sl3, X0[:], ident_f[:], reads=[X0k, 'ident_f'], writes=[sl3k])
                bk = pp[4 + rot[0]]; bkk = 'gs%d' % rot[0]
                I('act', 'activation', fl(B4['XT0']), bk[:, :], AF.Identity, reads=[bkk], writes=K4('XT0'))
                I('dve', 'tensor_tensor', fl(B4['TT']), fl(ident4), bk[:, :], ALU.subtract, reads=[bkk, 'ident4'], writes=K4('TT'))
                cur = 0
                for lvl in range(1, 6):
                    k.task('g')
                    nxt = cur ^ 1
                    for h in H4:
                        X, Xk = hb('X%d' % cur, h); XT, XTk = hb('XT%d' % cur, h)
                        sl, slk = nslot(h)
                        I('pe', 'matmul', sl, XT[:], X[:], start=True, stop=True, reads=[XTk, Xk], writes=[slk])
                    bk = pp[4 + rot[0]]; bkk = 'gs%d' % rot[0]
                    I('act', 'activation', fl(B4['X%d' % nxt]), bk[:, :], AF.Identity, reads=[bkk], writes=K4('X%d' % nxt))
                    if lvl < 5:
                        for h in H4:
                            X, Xk = hb('X%d' % cur, h); XT, XTk = hb('XT%d' % cur, h)
                            sl2, sl2k = nslot(h)
                            I('pe', 'matmul', sl2, X[:], XT[:], start=True, stop=True, reads=[XTk, Xk], writes=[sl2k])
                        bk = pp[4 + rot[0]]; bkk = 'gs%d' % rot[0]
                        I('dve', 'tensor_copy', fl(B4['XT%d' % nxt]), bk[:, :], reads=[bkk], writes=K4('XT%d' % nxt))
                    for h in H4:
                        Xn, Xnk = hb('X%d' % nxt, h); TT, TTk = hb('TT', h)
                        sl, slk = nslot(h)
                        I('pe', 'matmul', sl, Xn[:], TT[:], start=True, stop=True, reads=[Xnk, TTk], writes=[slk])
                    bk = pp[4 + rot[0]]; bkk = 'gs%d' % rot[0]
                    I('dve', 'tensor_tensor', fl(B4['TT']), fl(B4['TT']), bk[:, :], ALU.add, reads=[bkk] + K4('TT'), writes=K4('TT'))
                    cur = nxt
                k.task('g')
                I('act', 'activation', fl(B4['TTb']), fl(B4['TT']), AF.Identity, reads=K4('TT'), writes=K4('TTb'))
                for h in H4:
                    kn, knk = hb('knT', h)
                    I('pe', 'transpose', ptb[:, h, :], kn[:], ident_b[:], reads=[knk, 'ident_b'], writes=['ptbG'])
                for h in H4:
                    kbg, kbgk = hb('kbg', h)
                    I('act', 'activation', kbg[:], ptb[:, h, :], AF.Identity, scale=gst[:, 24 + h:25 + h], reads=['ptbG', 'gst'], writes=[kbgk])
                for h in H4:
                    kdec, kdk = hb('kdec', h)
                    I('dve', 'tensor_scalar', kdec[:], ptb[:, h, :], gst[:, 28 + h:29 + h], None, ALU.mult, reads=['ptbG', 'gst'], writes=[kdk])
                for h in H4:
                    I('pe', 'transpose', ptb[:, h, :], csv[:, h, :], ident_b[:], reads=['csv%d' % h, 'ident_b'], writes=['ptbG'])
                for h in H4:
                    vb, vbk = hb('vb', h)
                    I('act', 'activation', vb[:], ptb[:, h, :], AF.Identity, scale=gst[:, 4 + h:5 + h], reads=['ptbG', 'gst'], writes=[vbk])
                k.task('g')
                slU = [nslot(h) for h in H4]
                bkU = pp[4 + rot[0]]; bkUk = 'gs%d' % rot[0]
                for h in H4:
                    vb, vbk = hb('vb', h); TT, TTk = hb('TTb', h)
                    I('pe', 'matmul', slU[h][0], TT[:], vb[:], start=True, stop=True, reads=[TTk, vbk], writes=[slU[h][1]])
                I('act', 'activation', fl(B4['u']), bkU[:, :], AF.Identity, reads=[bkUk], writes=K4('u'))
                slW = [nslot(h) for h in H4]
                bkW = pp[4 + rot[0]]; bkWk = 'gs%d' % rot[0]
                for h in H4:
                    kbg, kbgk = hb('kbg', h); TT, TTk = hb('TTb', h)
                    I('pe', 'matmul', slW[h][0], kbg[:], TT[:], start=True, stop=True, reads=[kbgk, TTk], writes=[slW[h][1]])
                bkW3 = bkW[:, :].rearrange("p (a b) -> p a b", a=4)
                for c in range(2):
                    I('dve', 'tensor_copy', B4['wkT%d' % c][:, :, c * 64:(c + 1) * 64], bkW3[:, :, c * 64:(c + 1) * 64], reads=[bkWk], writes=K4('wkT%d' % c))
                for c in range(2):
                    k.task('g')
                    r0, r1 = c * 64, (c + 1) * 64
                    slV = [nslot(h) for h in H4]
                    bkV = pp[4 + rot[0]]; bkVk = 'gs%d' % rot[0]
                    for h in H4:
                        wk, wkk = hb('wkT%d' % c, h)
                        I('pe', 'matmul', slV[h][0], wk[:], Sb[:, h, :], start=True, stop=True, reads=[wkk, 'Sb_%d' % h], writes=[slV[h][1]])
                    I('dve', 'tensor_tensor', fl(B4['vnew'])[r0:r1, :], fl(B4['u'])[r0:r1, :], bkV[r0:r1, :], ALU.subtract, reads=K4('u') + [bkVk], writes=K4('vnew'))
                    if own:
                        for h in H4:
                            qd, qdk = hb('qdT%d' % c, h); QKT, QKTk = hb('QKT', h); vn, vnk = hb('vnew', h)
                            po = pp[6][:, h * 128:(h + 1) * 128]
                            I('pe', 'matmul', po, qd[:], Sb[:, h, :], start=True, stop=False, reads=[qdk, 'Sb_%d' % h], writes=['po_all'])
                            I('pe', 'matmul', po, QKT[r0:r1, :], vn[r0:r1, :], start=False, stop=True, reads=[QKTk, vnk], writes=['po_all'])
                        I('act', 'activation', osb[r0:r1, :], pp[6][r0:r1, :], AF.Identity, reads=['po_all'], writes=['osb'])
                    slS = [nslot(h) for h in H4]
                    for h in H4:
                        vn, vnk = hb('vnew', h); kdec, kdk = hb('kdec', h)
                        I('pe', 'matmul', slS[h][0], kdec[r0:r1, :], vn[r0:r1, :], start=True, stop=True, reads=[kdk, vnk], writes=[slS[h][1]])
                    for h in H4:
                        I('dve', 'scalar_tensor_tensor', S[:, h, :], S[:, h, :], gst[:, 32 + 4 * c + h:33 + 4 * c + h], slS[h][0], ALU.mult, ALU.add,
                          reads=[slS[h][1], 'gst', 'S_%d' % h], writes=['S_%d' % h])
                    I('act', 'activation', fl(Sb), fl(S), AF.Identity, reads=['S_%d' % h for h in H4], writes=['Sb_%d' % h for h in H4])
                k.task('g')
                if own:
                    for h in H4:
                        I('act', 'activation', junk2[:, 0:128], osb[:, h * 128:(h + 1) * 128], AF.Square, accum_out=gst[:, 40 + h:41 + h],
                          reads=['osb'], writes=['junk2', 'gst'])
                    rstd_from_ss(gst[:, 40:44], gst[:, 44:48], 128, ['gst'], ['gst'], gst[:, 40:44])
                    for h in H4:
                        I('dve', 'scalar_tensor_tensor', oa[:, h * 128:(h + 1) * 128], osb[:, h * 128:(h + 1) * 128], gst[:, 44 + h:45 + h],
                          zw[:, h * 128:(h + 1) * 128], ALU.mult, ALU.mult, reads=['osb', 'gst', 'zw'], writes=['oa'])
                    dma('sp', oa_d[oi], oa[:], reads=['oa'], writes=['oa_d%d' % oi])

                def small_rms(src_ap, n, col, srck):
                    I('act', 'activation', junk2[:, 0:n], src_ap, AF.Square, accum_out=mst[:, col:col + 1], reads=[srck], writes=['junk2', 'mst'])
                    rstd_from_ss(mst[:, col:col + 1], mst[:, col + 1:col + 2], n, ['mst'], ['mst'], mst[:, col:col + 1])
                    return mst[:, col + 1:col + 2]

                def rope_T(src_ap, srck, rstd_ap, gbc, gbck, out_ap, outk, mult):
                    I('dve', 'scalar_tensor_tensor', rin[:], src_ap, rstd_ap, gbc[:], ALU.mult, ALU.mult, reads=[srck, 'mst', gbck], writes=['rin'])
                    x1, x2 = rin[:, 0:32], rin[:, 32:64]
                    cb, sb_ = cosT[:, blk, :], sinT[:, blk, :]
                    I('pool', 'tensor_tensor', rop[:, 0, :], x1, cb, ALU.mult, reads=['rin', 'cosT'], writes=['rop'])
                    I('pool', 'tensor_tensor', rop[:, 1, :], x2, sb_, ALU.mult, reads=['rin', 'sinT'], writes=['rop'])
                    I('pool', 'tensor_tensor', rop[:, 2, :], x2, cb, ALU.mult, reads=['rin', 'cosT'], writes=['rop'])
                    I('pool', 'tensor_tensor', rop[:, 3, :], x1, sb_, ALU.mult, reads=['rin', 'sinT'], writes=['rop'])
                    I('pool', 'tensor_tensor', rop[:, 4, :], rop[:, 0, :], rop[:, 1, :], ALU.subtract, reads=['rop'], writes=['rop'])
                    I('pool', 'tensor_tensor', rop[:, 5, :], rop[:, 2, :], rop[:, 3, :], ALU.add, reads=['rop'], writes=['rop'])
                    I('dve', 'tensor_scalar', out_ap, rop[:, 4:6, :].rearrange("p a b -> p (a b)"), mult, None, ALU.mult, reads=['rop'], writes=[outk])

                k.task('m')
                r_ = small_rms(mlain[:, 384:640], 256, 0, 'mlain')
                I('act', 'activation', mb16[:, 0:256], mlain[:, 384:640], AF.Identity, scale=r_, reads=['mlain', 'mst'], writes=['mb16'])
                for kc in range(2):
                    I('pe', 'transpose', ptb[:, 4 + kc, :], mb16[:, kc * 128:(kc + 1) * 128], ident_b[:], reads=['mb16', 'ident_b'], writes=['ptbM'])
                for kc in range(2):
                    I('act', 'activation', ckvnT[:, kc, :], ptb[:, 4 + kc, :], AF.Identity, scale=kvnw[:, kc:kc + 1], reads=['ptbM', 'kvnw'], writes=['ckvnT'])
                k.task('m')
                r_ = small_rms(mlain[:, 640:704], 64, 2, 'mlain')
                rope_T(mlain[:, 640:704], 'mlain', r_, gkr, 'gkr', rout[:, 0, :], 'rout', 1.0)
                I('pe', 'transpose', ptb[0:64, 6, :], rout[:, 0, :], ident_b[:], reads=['rout', 'ident_b'], writes=['ptbM'])
                I('act', 'activation', krTt[:], ptb[0:64, 6, :], AF.Identity, reads=['ptbM'], writes=['krTt'])
                dma('sp', krT_d[blk], krTt[:], reads=['krTt'], writes=['krT_d%d' % blk])
                k.task('m')
                for half in range(2):
                    for kc in range(2):
                        I('pe', 'matmul', pp[1][:, :], ckvnT[:, kc, :], Wukv[:, kc, half * 512:(half + 1) * 512], start=(kc == 0), stop=(kc == 1),
                          reads=['ckvnT', 'Wukv'], writes=['pp1'])
                    for hh in range(2):
                        h = half * 2 + hh
                        kap = pp[1][:, hh * 256:hh * 256 + 128]
                        r_ = small_rms(kap, 128, 4 + 2 * hh, 'pp1')
                        I('act', 'activation', mb16[:, hh * 128:(hh + 1) * 128], kap, AF.Identity, scale=r_, reads=['pp1', 'mst'], writes=['mb16'])
                        I('pe', 'transpose', ptb[:, 4 + hh, :], mb16[:, hh * 128:(hh + 1) * 128], ident_b[:], reads=['mb16', 'ident_b'], writes=['ptbM'])
                        I('act', 'activation', KTt[:, h, :], ptb[:, 4 + hh, :], AF.Identity, scale=gkn[:, 0:1], reads=['ptbM', 'gkn'], writes=['KTt'])
                        I('dve', 'tensor_copy', Vt[:, h, 0:128], pp[1][:, hh * 256 + 128:hh * 256 + 256], reads=['pp1'], writes=['Vt'])
                k.task('m')
                dma('sp', KT_d[blk], KTt[:].rearrange("p a b -> p (a b)"), reads=['KTt'], writes=['KT_d%d' % blk])
                dma('sp', V_d[blk], Vt[:].rearrange("p a b -> p (a b)"), reads=['Vt'], writes=['V_d%d' % blk])
                if own:
                    k.task('m')
                    r_ = small_rms(mlain[:, 0:384], 384, 8, 'mlain')
                    I('act', 'activation', mb16[:, 0:384], mlain[:, 0:384], AF.Identity, scale=r_, reads=['mlain', 'mst'], writes=['mb16'])
                    for kc in range(3):
                        I('pe', 'transpose', ptb[:, 4 + kc, :], mb16[:, kc * 128:(kc + 1) * 128], ident_b[:], reads=['mb16', 'ident_b'], writes=['ptbM'])
                    for kc in range(3):
                        I('act', 'activation', cqnT[:, kc, :], ptb[:, 4 + kc, :], AF.Identity, scale=qnw[:, kc:kc + 1], reads=['ptbM', 'qnw'], writes=['cqnT'])
                    k.task('m')
                    for pair in range(2):
                        for kc in range(3):
                            I('pe', 'matmul', pp[1][:, 0:384], cqnT[:, kc, :], Wuq[:, kc, pair * 384:(pair + 1) * 384], start=(kc == 0), stop=(kc == 2),
                              reads=['cqnT', 'Wuq'], writes=['pp1'])
                        for hh in range(2):
                            h = pair * 2 + hh
                            nap = pp[1][:, hh * 192:hh * 192 + 128]; rap = pp[1][:, hh * 192 + 128:hh * 192 + 192]
                            r_ = small_rms(nap, 128, 10 + 4 * hh, 'pp1')
                            I('act', 'activation', mb16[:, hh * 128:(hh + 1) * 128], nap, AF.Identity, scale=r_, reads=['pp1', 'mst'], writes=['mb16'])
                            I('pe', 'transpose', ptb[:, 4 + hh, :], mb16[:, hh * 128:(hh + 1) * 128], ident_b[:], reads=['mb16', 'ident_b'], writes=['ptbM'])
                            I('act', 'activation', QTnt[:, h, :], ptb[:, 4 + hh, :], AF.Identity, scale=gqn[:, 0:1], reads=['ptbM', 'gqn'], writes=['QTnt'])
                            r2_ = small_rms(rap, 64, 12 + 4 * hh, 'pp1')
                            rope_T(rap, 'pp1', r2_, gqr, 'gqr', rout[:, 1 + hh, :], 'rout', SCALE_Q)
                            I('pe', 'transpose', ptb[0:64, 6 + hh, :], rout[:, 1 + hh, :], ident_b[:], reads=['rout', 'ident_b'], writes=['ptbM'])
                            I('act', 'activation', QTrt[:, h, :], ptb[0:64, 6 + hh, :], AF.Identity, reads=['ptbM'], writes=['QTrt'])
                    k.task('m')
                    dma('sp', QTn_d[oi], QTnt[:].rearrange("p a b -> p (a b)"), reads=['QTnt'], writes=['QTn_d%d' % oi])
                    dma('sp', QTr_d[oi], QTrt[:].rearrange("p a b -> p (a b)"), reads=['QTrt'], writes=['QTr_d%d' % oi])
                blocktasks.append(k.capture); k.capture = None; k.cur = None
            k.muted = False
            k.keymap = {}
            ST = [{tag: [c for t_, t in tasks if t_ == tag for c in t] for tag in 'pgm'} for tasks in blocktasks]
            if ST:
                k.replay(ST[0]['p'])
            for b_ in range(len(ST)):
                ss = [x for x in (ST[b_]['g'], ST[b_]['m'], ST[b_ + 1]['p'] if b_ + 1 < len(ST) else []) if x]
                k.replay(KB.merge(ss))
            k.es = es0
        k.barrier()
        if stop_after == 2:
            dbg['oa_d'] = (oa_d, [NOWN, 128, 512], BF16)
            dbg['KT_d'] = (KT_d, [NB, 128, 512], BF16)
            dbg['V_d'] = (V_d, [NB, 128, 520], BF16)
            dbg['krT_d'] = (krT_d, [NB, 64, 128], BF16)
            dbg['QTn_d'] = (QTn_d, [NOWN, 128, 512], BF16)
            dbg['QTr_d'] = (QTr_d, [NOWN, 64, 512], BF16)

        if stop_after >= 3:
          with ExitStack() as p3:
            es0 = k.es; k.es = p3
            KTa = sb("KTa", [128, NB, 512], BF16); Va = sb("Va", [128, NB, 520], BF16); krTa = sb("krTa", [64, NB, 128], BF16)
            allK = ['KT_d%d' % b for b in range(NB)]; allV = ['V_d%d' % b for b in range(NB)]; allR = ['krT_d%d' % b for b in range(NB)]
            for a in range(0, NB, 16):
                b_ = min(NB, a + 16)
                dma('sp', KTa[:, a:b_, :], KT_d[a:b_].rearrange("n p f -> p n f"), reads=allK, writes=['KTa'])
                dma('sp', Va[:, a:b_, :], V_d[a:b_].rearrange("n p f -> p n f"), reads=allV, writes=['Va'])
                dma('sp', krTa[:, a:b_, :], krT_d[a:b_].rearrange("n p f -> p n f"), reads=allR, writes=['krTa'])
            dmask = sb("dmask", [128, 512], F32); onw = sb("onw", [128, 512], F32)
            dma('sp', dmask[:], c_dmask, writes=['dmask']); dma('sp', onw[:], onw_bc, writes=['onw'])
            QTn = [sb("QTn%d" % i, [128, 4, 128], BF16) for i in range(2)]
            QTr = [sb("QTr%d" % i, [64, 4, 128], BF16) for i in range(2)]
            PT = [sb("PT%d" % i, [128, 512], BF16) for i in range(2)]
            otmp = sb("otmp", [128, 512], F32); ob = sb("ob", [128, 512], BF16); ast = sb("ast", [128, 16], F32); junk3 = sb("junk3", [128, 128], F32)
            psS = [ps("psS%d" % i, [128, 512], F32) for i in range(2)]
            po2 = [ps("po2_%d" % i, [128, 512], F32) for i in range(4)]
            cnt = 0
            for oi, j in enumerate(OWN):
                qn_, qnk = QTn[oi % 2], 'QTn%d' % (oi % 2); qr_, qrk = QTr[oi % 2], 'QTr%d' % (oi % 2)
                for o2 in ([0, 1] if oi == 0 else [oi + 1]):
                    if o2 < NOWN:
                        dma('sp', QTn[o2 % 2][:].rearrange("p a b -> p (a b)"), QTn_d[o2], reads=['QTn_d%d' % o2], writes=['QTn%d' % (o2 % 2)])
                        dma('sp', QTr[o2 % 2][:].rearrange("p a b -> p (a b)"), QTr_d[o2], reads=['QTr_d%d' % o2], writes=['QTr%d' % (o2 % 2)])
                def emit_qk(kb, b):
                    pS, pSk = psS[b], 'psS%d' % b
                    I('pe', 'matmul', pS[:, :], krTa[:, kb, :], qr_[:].rearrange("p a b -> p (a b)"), start=True, stop=False, reads=['krTa', qrk], writes=[pSk])
                    for h in range(4):
                        I('pe', 'matmul', pS[:, h * 128:(h + 1) * 128], KTa[:, kb, h * 128:(h + 1) * 128], qn_[:, h, :], start=False, stop=True,
                          reads=['KTa', qnk], writes=[pSk])
                emit_qk(0, cnt % 2)
                for kb in range(j + 1):
                    b = cnt % 2; cnt += 1
                    pS, pSk = psS[b], 'psS%d' % b; pt, ptk = PT[b], 'PT%d' % b
                    if kb + 1 <= j:
                        emit_qk(kb + 1, cnt % 2)
                    if kb == 0:
                        I('act', 'activation', pt[:], pS[:, :], AF.Exp, bias=flg[:, 1:2], reads=[pSk, 'flg'], writes=[ptk])
                    else:
                        I('act', 'activation', pt[:], pS[:, :], AF.Exp, reads=[pSk], writes=[ptk])
                    if kb == j:
                        I('dve', 'tensor_tensor', pt[:], pt[:], dmask[:], ALU.mult, reads=[ptk, 'dmask'], writes=[ptk])
                    for h in range(4):
                        I('pe', 'matmul', po2[h][:, 0:130], pt[:, h * 128:(h + 1) * 128], Va[:, kb, h * 130:(h + 1) * 130],
                          start=(kb == 0), stop=(kb == j), reads=[ptk, 'Va'], writes=['po2_%d' % h])
                for h in range(4):
                    pv_ = po2[h][:, 0:130]
                    I('dve', 'reciprocal', ast[:, h:h + 1], pv_[:, 128:129], reads=['po2_%d' % h], writes=['ast'])
                    I('dve', 'tensor_scalar', otmp[:, h * 128:(h + 1) * 128], pv_[:, 0:128], ast[:, h:h + 1], None, ALU.mult, reads=['po2_%d' % h, 'ast'], writes=['otmp'])
                    I('act', 'activation', junk3[:], otmp[:, h * 128:(h + 1) * 128], AF.Square, accum_out=ast[:, 4 + h:5 + h], reads=['otmp'], writes=['junk3', 'ast'])
                rstd_from_ss(ast[:, 4:8], ast[:, 8:12], 128, ['ast'], ['ast'], ast[:, 4:8])
                for h in range(4):
                    I('dve', 'scalar_tensor_tensor', ob[:, h * 128:(h + 1) * 128], otmp[:, h * 128:(h + 1) * 128], ast[:, 8 + h:9 + h],
                      onw[:, h * 128:(h + 1) * 128], ALU.mult, ALU.mult, reads=['otmp', 'ast', 'onw'], writes=['ob'])
                dma('sp', ob_d[oi], ob[:], reads=['ob'], writes=['ob_d%d' % oi])
            k.es = es0
        k.barrier()
        if stop_after == 3:
            dbg['ob_d'] = (ob_d, [NOWN, 128, 512], BF16)

        if stop_after >= 4:
          with ExitStack() as p4:
            es0 = k.es; k.es = p4
            Wi = sb("Wi2", [128, 8, 2 * DFF], BF16); Wo = sb("Wo2", [128, NFC, D], BF16); Wm = sb("Wm", [128, 8, D], BF16)
            load_w_bf16(Wm, 'Wm', w_o, 8, D)
            load_w_bf16(Wi, 'Wi2', w2i, 8, 2 * DFF)
            load_w_bf16(Wo, 'Wo2', w2o, NFC, D)
            gT = sb("gT4", [128, D], F32)
            dma('sp', gT[:], gB_d[1], reads=['gB_d1'], writes=['gT'])
            for kc in range(8):
                I('dve', 'tensor_tensor', Wm[:, kc, :], Wm[:, kc, :], gT[:], ALU.mult, reads=['Wm', 'gT'], writes=['Wm'])
            dma('sp', gT[:], gB_d[2], reads=['gB_d2'], writes=['gT'])
            for fc in range(NFC):
                I('dve', 'tensor_tensor', Wo[:, fc, :], Wo[:, fc, :], gT[:], ALU.mult, reads=['Wo2', 'gT'], writes=['Wo2'])
            xts = [sb("xg4_%d" % i, [128, G, D], F32) for i in range(2)]; mix = sb("mix", [128, G, D], BF16)
            hTs4 = [sb("hT4_%d" % i, [128, 8, GT_MAX], BF16) for i in range(2)]; aT = sb("aT4", [128, NFC, GT_MAX], BF16)
            xnf4 = [sb("xnf4_%d" % i, [128, D], BF16) for i in range(G)]
            sgs = [sb("sg4_%d" % i, [128, GT_MAX], F32) for i in range(2)]
            st4 = sb("st4", [128, 3 * G], F32)
            pTs4 = [ps("pT4_%d" % i, [128, 8, 128], BF16) for i in range(2)]
            pgs = [ps("pg4_%d" % i, [128, 512], F32) for i in range(2)]
            pus = [ps("pu4_%d" % i, [128, 512], F32) for i in range(2)]
            py = ps("py4", [128, D], F32)
            groups = [list(range(a, min(NOWN, a + G))) for a in range(0, NOWN, G)]
            NG = len(groups)

            def stats4(xap, xk, c0, xn_t, xnk):
                I('act', 'activation', xn_t[:], xap, AF.Square, accum_out=st4[:, c0:c0 + 1], reads=[xk], writes=[xnk, 'st4'])
                rstd_from_ss(st4[:, c0:c0 + 1], st4[:, c0 + 2:c0 + 3], D, ['st4'], ['st4'], st4[:, c0 + 1:c0 + 2])
                I('dve', 'tensor_scalar', xn_t[:], xap, st4[:, c0 + 2:c0 + 3], None, ALU.mult, reads=['st4', xk], writes=[xnk])

            k.capture_start()
            for gi_, ois in enumerate(groups):
                ng = len(ois); GT = ng * 128; o0 = ois[0]
                xt, xk = xts[gi_ % 2], 'xg4_%d' % (gi_ % 2)
                hTg, hTgk = hTs4[gi_ % 2], 'hT4_%d' % (gi_ % 2)
                k.task(('L', gi_))
                dma('sp', xt[:, 0:ng, :], x1_d[o0:o0 + ng].rearrange("n p d -> p n d"), reads=['x1_d%d' % o for o in ois], writes=[xk])
                dma('sp', mix[:, 0:ng, 0:512], oa_d[o0:o0 + ng].rearrange("n p d -> p n d"), reads=['oa_d%d' % o for o in ois], writes=['mix'])
                dma('sp', mix[:, 0:ng, 512:1024], ob_d[o0:o0 + ng].rearrange("n p d -> p n d"), reads=['ob_d%d' % o for o in ois], writes=['mix'])
                k.task(('Ma', gi_))
                for s in range(ng):
                    for kc in range(8):
                        I('pe', 'transpose', pTs4[s][:, kc, :], mix[:, s, kc * 128:(kc + 1) * 128], ident_b[:], reads=['mix', 'ident_b'], writes=['pT4_%d' % s])
                    I('act', 'activation', hTg[:, :, s * 128:(s + 1) * 128], pTs4[s][:, :, :], AF.Identity, reads=['pT4_%d' % s], writes=[hTgk])
                k.task(('Mb', gi_))
                for s in range(ng):
                    for half in range(2):
                        for kc in range(8):
                            I('pe', 'matmul', py[:, half * 512:(half + 1) * 512], hTg[:, kc, s * 128:(s + 1) * 128], Wm[:, kc, half * 512:(half + 1) * 512],
                              start=(kc == 0), stop=(kc == 7), reads=[hTgk, 'Wm'], writes=['py'])
                    I('dve', 'tensor_tensor', xt[:, s, :], xt[:, s, :], py[:], ALU.add, reads=['py', xk], writes=[xk])
                k.task(('Fa', gi_))
                for s in range(ng):
                    stats4(xt[:, s, :], xk, 3 * s, xnf4[s], 'xnf4_%d' % s)
                k.task(('Fb', gi_))
                for s in range(ng):
                    tr_evac(xnf4[s], 'xnf4_%d' % s, 2, pTs4[s], 'pT4_%d' % s, lambda kc, s=s, hTg=hTg: hTg[:, kc, s * 128:(s + 1) * 128], hTgk)
                k.task(('GU1', gi_))
                ffn_group(hTg, hTgk, GT, Wi, 'Wi2', aT, pgs, pus, sgs, gi_, fcs=range(0, NFC // 2))
                k.task(('GU2', gi_))
                ffn_group(hTg, hTgk, GT, Wi, 'Wi2', aT, pgs, pus, sgs, gi_, fcs=range(NFC // 2, NFC))
                k.task(('D', gi_))
                for s in range(ng):
                    ffn_down(aT, s, Wo, 'Wo2', py)
                    I('dve', 'tensor_tensor', xt[:, s, :], xt[:, s, :], py[:], ALU.add, reads=['py', xk], writes=[xk])
                    dma('sp', out[(o0 + s) * 128:(o0 + s + 1) * 128, :], xt[:, s, :], reads=[xk], writes=['out%d' % (o0 + s)])
            T4 = dict(k.capture); k.capture = None; k.cur = None
            order = []
            if NG:
                order += [('L', 0), ('Ma', 0), ('Mb', 0), ('Fa', 0), ('Fb', 0)]
                if NG > 1:
                    order.append(('L', 1))
            for g in range(NG):
                if g + 1 < NG:
                    order.append(('Ma', g + 1))
                order.append(('GU1', g))
                if g + 1 < NG:
                    order.append(('Mb', g + 1))
                order.append(('GU2', g))
                if g + 1 < NG:
                    order.append(('Fa', g + 1))
                order.append(('D', g))
                if g + 1 < NG:
                    order.append(('Fb', g + 1))
                if g + 2 < NG:
                    order.append(('L', g + 2))
            assert sorted(order) == sorted(T4.keys()), (len(order), len(T4))
            for key_ in order:
                k.replay(T4[key_])
            k.es = es0
        k.barrier()

        outkeys = ['out%d' % o for o in range(NOWN)]
        for nm, (src, shape, dt) in dbg.items():
            dd = nc.dram_tensor("dbg_" + nm, list(shape), dt, kind="ExternalOutput").ap()
            if nm == 'modT':
                dma('sp', dd, src[:], reads=['modT'], writes=['dbg_' + nm])
            else:
                allk = [kk_ for kk_ in k.last_w if isinstance(kk_, str) and kk_.startswith(nm)]
                dma('sp', dd, src, reads=allk, writes=['dbg_' + nm])
            outkeys.append('dbg_' + nm)
        k.finish(outkeys)
    return nc, list(dbg.keys())


def _consts():
    f = np.float32
    idx = np.arange(128)
    same = (idx[:, None] // 64) == (idx[None, :] // 64)
    c = {}
    c["c_ident"] = np.eye(128, dtype=f)
    c["c_utri"] = (same & (idx[:, None] <= idx[None, :])).astype(f)
    c["c_maskU"] = np.where(same & (idx[None, :] >= idx[:, None]), 0.0, NEG).astype(f)
    c["c_maskL"] = np.where(same & (idx[None, :] <= idx[:, None]), 0.0, -NEG).astype(f)
    c["c_bones"] = same.astype(f)
    cind = np.zeros((128, 256), f); cind[0:64, 0:128] = 1.0; cind[64:128, 128:256] = 1.0
    c["c_cind"] = cind
    invf = (10000.0 ** (-(np.arange(32, dtype=f) / f(32)))).astype(f)
    c["c_invf"] = np.broadcast_to((invf / f(TWO_PI)).astype(f)[None, :], (128, 32)).copy()
    dm = np.ones((128, 128), f); dm[64:128, 0:64] = 0.0
    c["c_dmask"] = np.tile(dm, (1, 4))
    return c


_PROG_CACHE = {}


def _get_prog(NB, stop_after=99):
    key = (NB, stop_after)
    if key not in _PROG_CACHE:
        _PROG_CACHE[key] = build_program(NB, stop_after=stop_after)
    return _PROG_CACHE[key]


def kernel(x, c, positions, w_ada, b_ada, ffn1_w_in, ffn1_w_out, w_in, gdn_conv_w, gdn_a_log, gdn_dt_bias, gdn_norm_w,
           mla_q_norm_w, mla_w_uq, mla_kv_norm_w, mla_w_ukv, qkn_q_nope, qkn_q_rope, qkn_k_nope, qkn_k_rope,
           mla_out_norm_w, w_out, ffn2_w_in, ffn2_w_out, _stop_after=99):
    f = np.float32
    A = lambda a: np.ascontiguousarray(np.asarray(a))
    x = A(x); B, S, _ = x.shape
    nblk = S // 128
    NB = nblk + 1
    NOWN = (NB + 1) // 2
    bc = lambda v, n=128: np.ascontiguousarray(np.broadcast_to(A(v).reshape(1, -1), (n, A(v).size))).astype(f)
    shared = dict(_consts())
    shared.update({
        "w_ada": A(w_ada)[0], "b_adaT": A(A(b_ada)[0].reshape(72, 128).T),
        "b_gate": A(np.concatenate([A(b_ada)[0][(3 * i + 2) * D:(3 * i + 3) * D] for i in range(3)])[None, :]),
        "ffn1_w_in": A(ffn1_w_in)[0], "ffn1_w_out": A(ffn1_w_out)[0], "ffn2_w_in": A(ffn2_w_in)[0], "ffn2_w_out": A(ffn2_w_out)[0],
        "w_in": A(w_in)[0], "conv_c": A(A(gdn_conv_w)[0].T.reshape(12, 128, 4).transpose(1, 0, 2)),
        "alog_bc": bc(A(gdn_a_log)[0]), "dtb_bc": bc(A(gdn_dt_bias)[0]),
        "gnw_bc": bc(np.tile(A(gdn_norm_w)[0], 4)), "qnw_col": A(A(mla_q_norm_w)[0].reshape(3, 128).T),
        "kvnw_col": A(A(mla_kv_norm_w)[0].reshape(2, 128).T), "w_uq": A(mla_w_uq)[0], "w_ukv": A(mla_w_ukv)[0],
        "gqn_col": A(A(qkn_q_nope)[0].reshape(128, 1)), "gkn_col": A(A(qkn_k_nope)[0].reshape(128, 1)),
        "gqr_bc": bc(A(qkn_q_rope)[0]), "gkr_bc": bc(A(qkn_k_rope)[0]),
        "onw_bc": bc(np.tile(A(mla_out_norm_w)[0], 4)), "w_out": A(w_out)[0],
    })
    in_maps = []
    pos = A(positions).astype(np.int32)
    for b in range(B):
        for p in range(2):
            xs = np.zeros((NB * 128, D), f)
            xs[p * 128:p * 128 + S] = x[b]
            ps_ = np.zeros((NB * 128,), np.int32)
            ps_[p * 128:p * 128 + S] = pos[b]
            m = dict(shared)
            m["xs"] = xs
            m["c_col"] = A(A(c)[b].reshape(8, 128).T)
            m["pos_col"] = A(ps_.reshape(NB, 128).T)
            fl = np.zeros((128, 2), f)
            fl[:, 0] = 1.0 if p == 0 else 0.0
            fl[:, 1] = 0.0 if p == 0 else NEG
            m["flags"] = fl
            in_maps.append(m)
    nc, dbgkeys = _get_prog(NB, _stop_after)
    res = run_bass_kernel_spmd(nc, in_maps, core_ids=list(range(len(in_maps))))
    if _stop_after != 99:
        return res.results
    outp = np.zeros((B, S, D), f)
    for b in range(B):
        for p in range(2):
            o = res.results[b * 2 + p]["out"].reshape(NOWN, 128, D)
            for oi in range(NOWN):
                blk = 2 * oi - p
                if 0 <= blk < nblk:
                    outp[b, blk * 128:(blk + 1) * 128] = o[oi]
    return outp
```
